# Optimizing a Trainium2 kernel written in Bass

```python
import math
import jax, jax.numpy as jnp
from jax import lax
import numpy as np

D_MODEL = 1024
BATCH = 8
SEQ = 2048
DEPTH = 1
DEC_BATCH = 128
DEC_SEQ = 8
PAST_LEN = 16384
PAGE_SIZE = 128

N_MEM = 256
D_A = 384
D_B = 384
N_XHEADS = 4
XHEAD_DIM = 64
D_X = N_XHEADS * XHEAD_DIM
D_MIX = D_A + D_B + D_X
CONV_A = 3
CONV_B = 31
SPLIT_SIZES = (D_A, D_A, D_A, D_A, D_B, D_B, D_B, D_X, D_X)
D_IN = 4 * D_A + 3 * D_B + 2 * D_X
EPS = 1e-6

kernel_name = "hybrid_shortconv_conformer_memxattn_step"


def rmsnorm(x, g):
    xf = x.astype(jnp.float32)
    out = xf * lax.rsqrt(jnp.mean(xf * xf, axis=-1, keepdims=True) + EPS)
    return (out * g.astype(jnp.float32)).astype(x.dtype)


def layernorm(x, g, b):
    xf = x.astype(jnp.float32)
    mu = jnp.mean(xf, axis=-1, keepdims=True)
    xc = xf - mu
    out = xc * lax.rsqrt(jnp.mean(xc * xc, axis=-1, keepdims=True) + EPS)
    return (out * g.astype(jnp.float32) + b.astype(jnp.float32)).astype(x.dtype)


def causal_dwconv(ctx, x, w, b):
    k = w.shape[0]
    xp = jnp.concatenate([ctx, x], axis=1)
    y = lax.conv_general_dilated(
        xp, w[:, None, :].astype(xp.dtype), window_strides=(1,), padding='VALID',
        dimension_numbers=('NWC', 'WIO', 'NWC'), feature_group_count=x.shape[-1])
    return y + b, xp[:, xp.shape[1] - (k - 1):]


def mem_kv(mem, g_mem, w_mk, w_mv):
    m = rmsnorm(mem, g_mem)
    bsz = mem.shape[0]
    k = (m @ w_mk).reshape(bsz, N_MEM, N_XHEADS, XHEAD_DIM)
    v = (m @ w_mv).reshape(bsz, N_MEM, N_XHEADS, XHEAD_DIM)
    return k, v


def cross_attn(q, k, v):
    scores = jnp.einsum('bshd,bmhd->bhsm', q, k).astype(jnp.float32) * (XHEAD_DIM ** -0.5)
    p = jax.nn.softmax(scores, axis=-1).astype(v.dtype)
    out = jnp.einsum('bhsm,bmhd->bshd', p, v)
    return out.reshape(q.shape[0], q.shape[1], D_X)


def mixer_layer(x, ctx_a, ctx_b, mk, mv, g_norm, w_in, w_conv_a, b_conv_a,
                w_conv_b, b_conv_b, ln_g, ln_b, w_out):
    bsz, slen, _ = x.shape
    h = rmsnorm(x, g_norm)
    proj = h @ w_in
    parts = []
    off = 0
    for sz in SPLIT_SIZES:
        parts.append(proj[..., off:off + sz])
        off += sz
    bg_a, cg_a, v_a, z_a, a_b, g_b, z_b, q_x, z_x = parts
    conv_a, new_ctx_a = causal_dwconv(ctx_a, cg_a * v_a, w_conv_a, b_conv_a)
    out_a = bg_a * conv_a * jax.nn.silu(z_a)
    u_b = a_b * jax.nn.sigmoid(g_b)
    conv_b, new_ctx_b = causal_dwconv(ctx_b, u_b, w_conv_b, b_conv_b)
    out_b = jax.nn.silu(layernorm(conv_b, ln_g, ln_b)) * jax.nn.silu(z_b)
    q = q_x.reshape(bsz, slen, N_XHEADS, XHEAD_DIM)
    out_x = cross_attn(q, mk, mv) * jax.nn.silu(z_x)
    mixed = jnp.concatenate([out_a, out_b, out_x], axis=-1)
    return x + mixed @ w_out, new_ctx_a, new_ctx_b


def setup_inputs(seed: int = 0) -> dict:
    key = jax.random.key(seed)
    ks = jax.random.split(key, 24)
    f32 = jnp.float32
    nrm = lambda k, shape, s: jax.random.normal(k, shape, f32) * s
    return {
        "x_prompt": nrm(ks[0], (BATCH, SEQ, D_MODEL), 1.0),
        "x_sample": nrm(ks[1], (DEC_BATCH, DEC_SEQ, D_MODEL), 1.0),
        "state_conv_a": nrm(ks[2], (DEPTH, DEC_BATCH, CONV_A - 1, D_A), 1.0),
        "state_conv_b": nrm(ks[3], (DEPTH, DEC_BATCH, CONV_B - 1, D_B), 0.5),
        "cache_mem_k": nrm(ks[4], (DEPTH, DEC_BATCH, N_MEM, N_XHEADS, XHEAD_DIM), 1.0),
        "cache_mem_v": nrm(ks[5], (DEPTH, DEC_BATCH, N_MEM, N_XHEADS, XHEAD_DIM), 1.0),
        "mem_prompt": nrm(ks[6], (BATCH, N_MEM, D_MODEL), 1.0),
        "g_norm": 1.0 + nrm(ks[7], (DEPTH, D_MODEL), 0.02),
        "w_in": nrm(ks[8], (DEPTH, D_MODEL, D_IN), D_MODEL ** -0.5),
        "w_conv_a": nrm(ks[9], (DEPTH, CONV_A, D_A), CONV_A ** -0.5),
        "b_conv_a": nrm(ks[10], (DEPTH, D_A), 0.01),
        "w_conv_b": nrm(ks[11], (DEPTH, CONV_B, D_B), CONV_B ** -0.5),
        "b_conv_b": nrm(ks[12], (DEPTH, D_B), 0.01),
        "ln_g": 1.0 + nrm(ks[13], (DEPTH, D_B), 0.02),
        "ln_b": nrm(ks[14], (DEPTH, D_B), 0.01),
        "w_out": nrm(ks[15], (DEPTH, D_MIX, D_MODEL), D_MIX ** -0.5),
        "g_mem": 1.0 + nrm(ks[16], (DEPTH, D_MODEL), 0.02),
        "w_mem_k": nrm(ks[17], (DEPTH, D_MODEL, D_X), D_MODEL ** -0.5),
        "w_mem_v": nrm(ks[18], (DEPTH, D_MODEL, D_X), D_MODEL ** -0.5),
        "g_final": 1.0 + nrm(ks[19], (D_MODEL,), 0.02),
    }


def reference(x_prompt, x_sample, state_conv_a, state_conv_b, cache_mem_k, cache_mem_v,
              mem_prompt, g_norm, w_in, w_conv_a, b_conv_a, w_conv_b, b_conv_b,
              ln_g, ln_b, w_out, g_mem, w_mem_k, w_mem_v, g_final):
    yp, ys = x_prompt, x_sample
    bsz = x_prompt.shape[0]
    pa_list, pb_list, pk_list, pv_list, sa_list, sb_list = [], [], [], [], [], []
    for l in range(DEPTH):
        params = (g_norm[l], w_in[l], w_conv_a[l], b_conv_a[l], w_conv_b[l], b_conv_b[l],
                  ln_g[l], ln_b[l], w_out[l])
        mk_p, mv_p = mem_kv(mem_prompt, g_mem[l], w_mem_k[l], w_mem_v[l])
        ctx0_a = jnp.zeros((bsz, CONV_A - 1, D_A), yp.dtype)
        ctx0_b = jnp.zeros((bsz, CONV_B - 1, D_B), yp.dtype)
        yp, pa, pb = mixer_layer(yp, ctx0_a, ctx0_b, mk_p, mv_p, *params)
        ys, sa, sb = mixer_layer(ys, state_conv_a[l], state_conv_b[l],
                                 cache_mem_k[l], cache_mem_v[l], *params)
        pa_list.append(pa); pb_list.append(pb); pk_list.append(mk_p); pv_list.append(mv_p)
        sa_list.append(sa); sb_list.append(sb)
    y_prompt = rmsnorm(yp, g_final)
    y_sample = rmsnorm(ys, g_final)
    return (y_prompt, y_sample,
            jnp.stack(pa_list), jnp.stack(pb_list), jnp.stack(pk_list), jnp.stack(pv_list),
            jnp.stack(sa_list), jnp.stack(sb_list))
```

```python
import os
import numpy as np
import concourse.bass as bass
import concourse.mybir as mybir
from concourse.bass_utils import run_bass_kernel_spmd
from contextlib import ExitStack

F32 = mybir.dt.float32
BF16 = mybir.dt.bfloat16
AF = mybir.ActivationFunctionType
ALU = mybir.AluOpType

ENGS = ("pe", "act", "dve", "pool", "sp")
EPS = 1e-6
NCORES = 8


class Buf:
    __slots__ = ("name", "w", "r", "rel", "psum")

    def __init__(self, name, psum=False):
        self.name = name
        self.w = None
        self.r = []
        self.rel = 0
        self.psum = psum


class FW:
    NDMA = 48

    def __init__(self, nc, es):
        self.nc = nc
        self.es = es
        self.q = {e: [] for e in ENGS}
        self.cnt = {e: 0 for e in ENGS}
        self.sem = {}
        for e in ("pe", "act", "dve", "pool"):
            self.sem[e] = es.enter_context(nc.semaphore("s_" + e))
        self.dsem = [es.enter_context(nc.semaphore("d%d" % i)) for i in range(self.NDMA)]
        self.dval = [0] * self.NDMA
        self.dnext = {"sw": 0, "hw": 0}
        self.dbase = {"sw": (0, self.NDMA // 2), "hw": (self.NDMA // 2, self.NDMA - self.NDMA // 2)}
        self.waited = {e: {} for e in ENGS}
        self.seq = 0
        self.store_tickets = []

    def _semof(self, key):
        return self.sem[key] if isinstance(key, str) else self.dsem[key[1]]

    def _need(self, eng, tickets):
        best = {}
        for t in tickets:
            if t is None:
                continue
            k, v = t
            if k == eng and eng == "pe":
                continue
            if best.get(k, 0) < v:
                best[k] = v
        out = []
        for k, v in best.items():
            if self.waited[eng].get(k, 0) >= v:
                continue
            self.waited[eng][k] = v
            out.append((k, v))
        return out

    def _deps(self, eng, reads, writes):
        ts = []
        for b in reads:
            ts.append(b.w)
            if b.psum:
                for t in b.r:
                    if t[0] != eng:
                        ts.append(t)
        for b in writes:
            ts.append(b.w)
            for t in b.r:
                if t[0] == eng:
                    continue
                ts.append(t)
        return self._need(eng, ts)

    def _register(self, ticket, reads, writes):
        self.seq += 1
        for b in reads:
            b.r.append(ticket)
            b.rel = self.seq
        for b in writes:
            b.w = ticket
            b.r = []
            b.rel = self.seq

    def op(self, eng, fn, reads=(), writes=(), inc=True):
        waits = self._deps(eng, reads, writes)
        if inc:
            self.cnt[eng] += 1
            ticket = (eng, self.cnt[eng])
        else:
            ticket = (eng, self.cnt[eng] + 1)
        self._register(ticket, reads, writes)
        sems = [(self._semof(k), v) for k, v in waits]
        mysem = self.sem[eng]

        def thunk(e):
            for s, v in sems[1:]:
                e.wait_ge(s, v)
            ins = fn(e)
            if sems:
                ins._wait_ge(sems[0][0], sems[0][1])
            if inc:
                ins.then_inc(mysem, 1)
        self.q[eng].append(thunk)
        return ticket

    def dma(self, eng, out, in_, reads=(), writes=(), store=False, **kw):
        kind = "sw" if eng == "pool" else "hw"
        base, n = self.dbase[kind]
        idx = base + self.dnext[kind]
        self.dnext[kind] = (self.dnext[kind] + 1) % n
        prev = self.dval[idx]
        extra = [(("d", idx), prev)] if prev > 0 else []
        waits = self._deps(eng, reads, writes) + self._need(eng, extra)
        self.dval[idx] += 16
        ticket = (("d", idx), self.dval[idx])
        self._register(ticket, reads, writes)
        sems = [(self._semof(k), v) for k, v in waits]
        dsem = self.dsem[idx]

        def thunk(e):
            for s, v in sems:
                e.wait_ge(s, v)
            e.dma_start(out=out, in_=in_, **kw).then_inc(dsem, 16)
        self.q[eng].append(thunk)
        if store:
            self.store_tickets.append(ticket)
        return ticket

    def finish(self):
        nc = self.nc
        final = self._need("sp", self.store_tickets)
        sems = [(self._semof(k), v) for k, v in final]

        def fthunk(e):
            for s, v in sems:
                e.wait_ge(s, v)
        self.q["sp"].append(fthunk)
        block = self.es.enter_context(nc.Block())
        q = self.q

        @block.sync
        def _(e):
            for t in q["sp"]:
                t(e)

        @block.tensor
        def _(e):
            for t in q["pe"]:
                t(e)

        @block.scalar
        def _(e):
            for t in q["act"]:
                t(e)

        @block.vector
        def _(e):
            for t in q["dve"]:
                t(e)

        @block.gpsimd
        def _(e):
            for t in q["pool"]:
                t(e)


class LRU:
    def __init__(self, fw, tiles, life=None, name=""):
        self.fw = fw
        self.tiles = tiles
        self.life = life
        self.name = name
        self.allocs = []
        self.n = 0
        self.busy_until = [0] * len(tiles)

    def get(self):
        self.fw.seq += 1
        i = self.n
        self.n += 1
        if self.life is None:
            t = self.tiles[i % len(self.tiles)]
            b = Buf("rec", psum=t[1].psum)
            self.allocs.append(b)
            return (t[0], b)
        now = self.fw.seq
        cands = [k for k in range(len(self.tiles)) if self.busy_until[k] < now]
        if not cands:
            raise RuntimeError("pool %s exhausted at alloc %d" % (self.name, i))
        k = min(cands, key=lambda k: self.busy_until[k])
        self.busy_until[k] = max(self.life[i], now)
        return self.tiles[k]

    def lifetimes(self):
        return [b.rel for b in self.allocs]


_SUB = int(os.environ.get('KSUB', '9'))

O_BG, O_CG, O_V, O_ZA, O_AB, O_GB, O_ZB, O_Q, O_ZX = 0, 384, 768, 1152, 1536, 1920, 2304, 2688, 2944
D_IN = 3200


def build_nc(stage=99):
    life = _build(stage, None)
    return _build(stage, life)


def _build(stage, life):
    nc = bass.Bass("TRN2", target_bir_lowering=False)

    def din(name, shape):
        return nc.dram_tensor(name, list(shape), F32, kind="ExternalInput").ap()

    def dout(name, shape):
        return nc.dram_tensor(name, list(shape), F32, kind="ExternalOutput").ap()

    xp = din("xp", (2048, 1024))
    xs = din("xs", (128, 1024))
    sca = din("sca", (32, 384))
    scb = din("scb", (16, 30, 384))
    ck = din("ck", (16, 256, 256))
    cv = din("cv", (16, 256, 256))
    mem = din("mem", (256, 1024))
    g_norm = din("g_norm", (1024,))
    w_in = din("w_in", (1024, D_IN))
    w_conv_a = din("w_conv_a", (3, 384))
    b_conv_a = din("b_conv_a", (384,))
    w_conv_b = din("w_conv_b", (31, 384))
    b_conv_b = din("b_conv_b", (384,))
    ln_g = din("ln_g", (384,))
    ln_b = din("ln_b", (384,))
    w_out = din("w_out", (1024, 1024))
    g_mem = din("g_mem", (1024,))
    w_mem_k = din("w_mem_k", (1024, 256))
    w_mem_v = din("w_mem_v", (1024, 256))
    g_final = din("g_final", (1024,))
    ident_d = din("ident", (128, 128))

    yp = dout("yp", (2048, 1024))
    ys = dout("ys", (128, 1024))
    o_pa = dout("o_pa", (2, 384))
    o_pb = dout("o_pb", (30, 384))
    o_pk = dout("o_pk", (256, 256))
    o_pv = dout("o_pv", (256, 256))
    o_sa = dout("o_sa", (32, 384))
    o_sb = dout("o_sb", (16, 30, 384))

    with ExitStack() as es:
        fw = FW(nc, es)

        def T(name, shape, dt):
            return es.enter_context(nc.sbuf_tensor(name, list(shape), dt))

        pools = {}

        def mkpool(prefix, n, shape, dt):
            p = LRU(fw, [(T("%s%d" % (prefix, i), shape, dt), Buf("%s%d" % (prefix, i))) for i in range(n)],
                    None if life is None else life[prefix], prefix)
            pools[prefix] = p
            return p

        win = T("win", (128, 8, D_IN), BF16); b_win = {(g, k): Buf("win%d_%d" % (g, k)) for g in range(6) for k in range(8)}
        wout = T("wout", (128, 8, 1024), BF16); b_wout = [Buf("wout%d" % k) for k in range(8)]
        dgb = T("dgb", (128, 3, 31, 128), BF16); b_dgb = [Buf("dgb%d" % j) for j in range(3)]
        gfin = T("gfin", (128, 1024), F32); b_gfin = Buf("gfin")
        idf = T("idf", (128, 128), F32); b_idf = Buf("idf")
        idb = T("idb", (128, 128), BF16); b_idb = Buf("idb")
        ones = T("ones", (128, 128), BF16); b_ones = Buf("ones")
        gn = T("gn", (128, 8), F32); b_gn = Buf("gn")
        gm = T("gm", (128, 8), F32); b_gm = Buf("gm")
        bcb = T("bcb", (128, 3), F32); lng = T("lng", (128, 3), F32); lnb = T("lnb", (128, 3), F32)
        bca = T("bca", (128, 3), F32); wca = T("wca", (128, 3, 3), F32); wcb = T("wcb", (128, 3, 31), F32)
        nhalf = T("nhalf", (128, 8), F32); b_nhalf = Buf("nhalf")
        expw = T("expw", (128, 2), F32); b_expw = Buf("expw")
        b_gate = Buf("gate")
        hT = [T("hT%d" % i, (128, 8, 512), BF16) for i in range(2)]; b_hT = [Buf("hT0"), Buf("hT1")]
        ubuf = T("ubuf", (128, 3, 544), BF16); b_ub = [Buf("ub%d" % j) for j in range(3)]
        outT = T("outT", (128, 8, 512), BF16); b_outT = [Buf("outT%d" % j) for j in range(8)]
        wkv = hT[0]
        b_wkv = [b_hT[0]]
        cvh = T("cvh", (128, 3, 2), F32); b_cvh = Buf("cvh")
        cvh_s = T("cvh_s", (128, 3, 32), F32); b_cvh_s = Buf("cvh_s")
        ubuf_s = T("ubuf_s", (128, 3, 608), BF16); b_ub_s = [Buf("ubs%d" % j) for j in range(3)]
        u32 = T("u32", (128, 3, 128), F32); b_u32 = Buf("u32")
        KTz = T("KTz", (128, 4, 256), BF16); b_KTp = Buf("KTz")
        Vz = T("Vz", (128, 2, 4, 128), BF16); b_Vp = Buf("Vz")
        onesz = T("onesz", (128, 2, 128), BF16); b_onesz = Buf("onesz")
        qT = T("qT", (128, 2, 512), BF16); b_qT = [Buf("qT0"), Buf("qT1")]
        ssn = T("ssn", (128, 8), F32); b_ssn = [Buf("ssn%d" % i) for i in range(8)]
        rsn = T("rsn", (128, 8), F32); b_rsn = [Buf("rsn%d" % i) for i in range(8)]
        ss2 = T("ss2", (128, 8), F32); rs2 = T("rs2", (128, 8), F32)
        b_ss2 = [Buf("ss2_%d" % i) for i in range(8)]; b_rs2 = [Buf("rs2_%d" % i) for i in range(8)]
        lnt = T("lnt", (128, 4, 4), F32); b_lnt = Buf("lnt")
        szx_s = T("szx_s", (128, 2, 128), F32); b_szx_s = Buf("szx_s")
        qTs = T("qTs", (128, 2, 128), BF16); b_qTs = [Buf("qTs0"), Buf("qTs1")]
        OD_s = T("OD_s", (128, 4, 128), F32); b_OD_s = Buf("OD_s")
        outT_s = T("outT_s", (128, 8, 128), BF16); b_outT_s = [Buf("outTs%d" % j) for j in range(8)]

        xpool = mkpool("xq", 5, (128, 1024), F32)
        s32 = mkpool("sc", 10, (128, 512), F32)
        cvp = mkpool("cvb", 2, (128, 514), F32)
        s16 = mkpool("sb", 12, (128, 1024), BF16)
        banks = LRU(fw, [(es.enter_context(nc.psum_tensor("pb%d" % i, [128, 512], F32)), Buf("pb%d" % i, psum=True)) for i in range(8)],
                    None if life is None else life["banks"], "banks")
        pools["banks"] = banks

        op = fw.op
        dma = fw.dma

        dma("sp", idf[:], ident_d[:, :], writes=[b_idf])
        smalls = [Buf("smalls")]
        dma("act", gfin[:], g_final.partition_broadcast(128), writes=[b_gfin])

        def load_small_consts():
            stg, bstg = xpool.get()
            bs = [bstg] + [Buf("stg%d" % i) for i in range(7)]
            for r, d_ in ((0, b_conv_b), (1, ln_g), (2, ln_b), (3, b_conv_a)):
                dma("sp", stg[r:r + 1, 0:384], d_.rearrange("(o c) -> o c", o=1), writes=[bs[r]])
            dma("sp", stg[4:7, 0:384], w_conv_a[:, :], writes=[bs[4]])
            dma("sp", stg[7:38, 0:384], w_conv_b[:, :], writes=[bs[5]])
            dma("sp", stg[38:46, 0:128], g_norm.rearrange("(k p) -> k p", p=128), writes=[bs[6]])
            dma("sp", stg[46:54, 0:128], g_mem.rearrange("(k p) -> k p", p=128), writes=[bs[7]])
            pb, bpb = banks.get()
            offs = (0, 54, 92)
            for j in range(3):
                R = 54 if j == 0 else 38
                op("pe", lambda e, j=j, R=R: e.transpose(out=pb[:, offs[j]:offs[j] + R], in_=stg[0:R, j * 128:(j + 1) * 128],
                                                         identity=idf[0:R, 0:R]), reads=bs + [b_idf], writes=[bpb], inc=(j == 2))
            cst, bcst = s32.get()
            op("dve", lambda e: e.tensor_copy(out=cst[:, 0:130], in_=pb[:, 0:130]), reads=[bpb], writes=[bcst])

            def col(j, r):
                return offs[j] + r
            for j in range(3):
                for t_, r in ((bcb, 0), (lng, 1), (lnb, 2), (bca, 3)):
                    op("dve", lambda e, t_=t_, j=j, r=r: e.tensor_copy(out=t_[:, j:j + 1], in_=cst[:, col(j, r):col(j, r) + 1]),
                       reads=[bcst], writes=[smalls[0]])
                op("dve", lambda e, j=j: e.tensor_copy(out=wca[:, j, :], in_=cst[:, col(j, 4):col(j, 7)]), reads=[bcst], writes=[smalls[0]])
                op("dve", lambda e, j=j: e.tensor_copy(out=wcb[:, j, :], in_=cst[:, col(j, 7):col(j, 38)]), reads=[bcst], writes=[smalls[0]])
            op("dve", lambda e: e.tensor_copy(out=gn[:], in_=cst[:, 38:46]), reads=[bcst], writes=[b_gn])
            op("dve", lambda e: e.tensor_copy(out=gm[:], in_=cst[:, 46:54]), reads=[bcst], writes=[b_gm])

        op("pool", lambda e: e.memset(nhalf[:], -0.5), writes=[b_nhalf])
        op("pool", lambda e: e.memset(ones[:], 1.0), writes=[b_ones])
        op("dve", lambda e: e.tensor_copy(out=idb[:], in_=idf[:]), reads=[b_idf], writes=[b_idb])

        cast_rr = [0]

        def cast_scaled(dst, src, sc_ap, mul, reads, writes):
            if sc_ap is None:
                eng = "dve" if (cast_rr[0] % 2 == 0) else "act"
                cast_rr[0] += 1
                if eng == "dve":
                    op(eng, lambda e: e.tensor_copy(out=dst, in_=src), reads=reads, writes=writes)
                else:
                    op(eng, lambda e: e.activation(out=dst, in_=src, func=AF.Copy), reads=reads, writes=writes)
            else:
                op("dve", lambda e: e.tensor_scalar(out=dst, in0=src, scalar1=sc_ap, scalar2=float(mul),
                                                    op0=ALU.mult, op1=ALU.mult), reads=reads, writes=writes)

        b_wk, b_wv = Buf("wk"), Buf("wv")

        def issue_wkv(gate=()):
            for wi, (wsrc, bw) in enumerate(((w_mem_k, b_wk), (w_mem_v, b_wv))):
                dma("pool", wkv[:, :, wi * 256:(wi + 1) * 256], wsrc.rearrange("(k p) c -> p k c", p=128), reads=list(gate),
                    writes=[bw], max_dma_last_dim=4096)

        def scale_wkv():
            for kc in range(8):
                op("dve", lambda e, kc=kc: e.tensor_scalar(out=wkv[:, kc, :], in0=wkv[:, kc, :], scalar1=gm[:, kc:kc + 1], scalar2=1.0,
                                                          op0=ALU.mult, op1=ALU.mult), reads=[b_wk, b_wv, b_gm], writes=b_wkv)

        WIN_GROUPS = [
            (O_GB, O_ZB, [(O_GB, O_ZB, 1.0)]),
            (O_AB, O_GB, [(O_AB, O_GB, 0.5)]),
            (O_ZB, O_Q, [(O_ZB, O_Q, 1.0)]),
            (0, O_V, [(0, O_V, 1.0)]),
            (O_V, O_AB, [(O_V, O_AB, 1.0)]),
            (O_Q, D_IN, [(O_Q, O_ZX, 0.125), (O_ZX, D_IN, 1.0)]),
        ]
        w_in_v = w_in.rearrange("(k p) c -> p k c", p=128)

        def issue_win(groups, gate=()):
            for gi in groups:
                c0, c1, subs = WIN_GROUPS[gi]
                dma("pool", win[:, :, c0:c1], w_in_v[:, :, c0:c1], reads=list(gate), writes=[b_win[(gi, kc)] for kc in range(8)],
                    max_dma_last_dim=4096)

        def scale_win(groups):
            for gi in groups:
                c0, c1, subs = WIN_GROUPS[gi]
                for kc in range(8):
                    for (a0, a1, mul) in subs:
                        op("dve", lambda e, kc=kc, a0=a0, a1=a1, mul=mul: e.tensor_scalar(
                            out=win[:, kc, a0:a1], in0=win[:, kc, a0:a1], scalar1=gn[:, kc:kc + 1], scalar2=float(mul),
                            op0=ALU.mult, op1=ALU.mult), reads=[b_win[(gi, kc)], b_gn], writes=[b_win[(gi, kc)]])

        def win_group(col):
            for gi, (c0, c1, subs) in enumerate(WIN_GROUPS):
                if c0 <= col < c1:
                    return gi
            raise ValueError(col)

        def load_wout(gate=()):
            dma("pool", wout[:, :, :], w_out.rearrange("(k p) c -> p k c", p=128), reads=list(gate), writes=list(b_wout),
                max_dma_last_dim=4096)

        def prep_load(src_rows, ntile, gate=()):
            xts = []
            for tt in range(ntile):
                xt, bxt = xpool.get()
                dma("sp", xt[:], src_rows(tt), reads=list(gate), writes=[bxt])
                xts.append((xt, bxt))
            return xts

        def prep_a(src_rows, ntile, xts=None):
            xhs = []
            if xts is None:
                xts = prep_load(src_rows, ntile)
            for tt in range(ntile):
                xt, bxt = xts[tt]
                xh, bxh = s16.get()
                op("act", lambda e, xt=xt, xh=xh, tt=tt: e.activation(out=xh[:], in_=xt[:], func=AF.Square,
                                                                       accum_out=ssn[:, tt:tt + 1]),
                   reads=[bxt], writes=[bxh, b_ssn[tt]])
                op("pool", lambda e, tt=tt: e.tensor_scalar(out=rsn[:, tt:tt + 1], in0=ssn[:, tt:tt + 1], scalar1=1.0 / 1024,
                                                            scalar2=EPS, op0=ALU.mult, op1=ALU.add),
                   reads=[b_ssn[tt]], writes=[b_rsn[tt]])
                op("pool", lambda e, tt=tt: e.tensor_tensor(out=rsn[:, tt:tt + 1], in0=rsn[:, tt:tt + 1], in1=nhalf[:, 0:1], op=ALU.pow),
                   reads=[b_rsn[tt], b_nhalf], writes=[b_rsn[tt]])
                op("dve", lambda e, xt=xt, xh=xh, tt=tt: e.tensor_scalar(out=xh[:], in0=xt[:], scalar1=rsn[:, tt:tt + 1],
                                                                         scalar2=None, op0=ALU.mult),
                   reads=[bxt, b_rsn[tt]], writes=[bxh])
                xhs.append((xh, bxh))
            return xhs

        def prep_b(xhs, dstT, b_dstT, col0):
            for tt, (xh, bxh) in enumerate(xhs):
                pb, bpb = banks.get()
                pbv = pb[:].bitcast(BF16).rearrange("p (k c) -> p k c", c=128)
                for kc in range(8):
                    op("pe", lambda e, pbv=pbv, xh=xh, kc=kc: e.transpose(out=pbv[:, kc, :], in_=xh[:, kc * 128:(kc + 1) * 128],
                                                                         identity=idb[:]),
                       reads=[bxh, b_idb], writes=[bpb], inc=(kc == 7))
                c = col0 + tt * 128
                op("act", lambda e, pbv=pbv, c=c: e.activation(out=dstT[:, :, c:c + 128], in_=pbv, func=AF.Copy),
                   reads=[bpb], writes=[b_dstT])

        def prep_tiles(src_rows, ntile, dstT, b_dstT, col0):
            prep_b(prep_a(src_rows, ntile), dstT, b_dstT, col0)

        def proj(col, hTb, b_hTb, nt, M=128):
            pb, bpb = banks.get()
            gi = win_group(col)
            for kc in range(8):
                op("pe", lambda e, pb=pb, kc=kc: e.matmul(pb[0:M, 0:nt], lhsT=win[:, kc, col:col + M], rhs=hTb[:, kc, 0:nt],
                                                         start=(kc == 0), stop=(kc == 7)),
                   reads=[b_win[(gi, kc)], b_hTb], writes=[bpb], inc=(kc == 7))
            return pb, bpb


        class Ctx:
            pass

        def seqv(ap, width):
            return ap.rearrange("p (s w) -> p s w", w=width)

        def new_ctx(hTb, b_hTb, nt, sample, last, oT, b_oT, qTd, b_qTd, ub=None, b_ub_=None, cvh_=None, b_cvh_=None):
            c = Ctx()
            c.hTb, c.b_hTb, c.nt, c.sample, c.last = hTb, b_hTb, nt, sample, last
            c.oT, c.b_oT, c.qT, c.b_qT = oT, b_oT, qTd, b_qTd
            c.ntile = nt // 128
            c.S = 16 if sample else 1
            c.L = nt // c.S
            c.ub = ubuf if ub is None else ub
            c.b_ub = b_ub if b_ub_ is None else b_ub_
            c.cvh = cvh if cvh_ is None else cvh_
            c.b_cvh = b_cvh if b_cvh_ is None else b_cvh_
            c.szb, c.y32, c.ysq, c.szx = [], [], [], []
            c.hook = lambda: None
            c.gate = None
            c.tdve = T_DVE
            return c

        def phase_B_proj(c):
            nt = c.nt
            ths = []
            for j in range(3):
                pg, bpg = proj(O_GB + 128 * j, c.hTb, c.b_hTb, nt)
                th, bth = s32.get()
                op("act", lambda e, pg=pg, th=th: e.activation(out=th[:, 0:nt], in_=pg[:, 0:nt], func=AF.Tanh, scale=0.5),
                   reads=[bpg], writes=[bth])
                ths.append((th, bth))
            for j in range(3):
                th, bth = ths[j]
                pa, bpa = proj(O_AB + 128 * j, c.hTb, c.b_hTb, nt)
                if c.sample:
                    udst = c.ub[:, j, :].rearrange("p (s w) -> p s w", w=38)[:, :, 30:38]
                    op("dve", lambda e, th=th, pa=pa, udst=udst: e.scalar_tensor_tensor(
                        out=udst, in0=seqv(th[:, 0:nt], 8), scalar=1.0, in1=seqv(pa[:, 0:nt], 8), op0=ALU.add, op1=ALU.mult),
                       reads=[bth, bpa], writes=[c.b_ub[j]])
                    op("dve", lambda e, th=th, pa=pa, j=j: e.scalar_tensor_tensor(
                        out=u32[:, j, 0:128], in0=th[:, 0:128], scalar=1.0, in1=pa[:, 0:128], op0=ALU.add, op1=ALU.mult),
                       reads=[bth, bpa], writes=[b_u32])
                else:
                    op("dve", lambda e, th=th, pa=pa, j=j: e.scalar_tensor_tensor(
                        out=c.ub[:, j, 30:30 + nt], in0=th[:, 0:nt], scalar=1.0, in1=pa[:, 0:nt], op0=ALU.add, op1=ALU.mult),
                       reads=[bth, bpa], writes=[c.b_ub[j]])
                    if c.last:
                        op("dve", lambda e, th=th, pa=pa, j=j: e.scalar_tensor_tensor(
                            out=u32[:, j, 0:32], in0=th[:, nt - 32:nt], scalar=1.0, in1=pa[:, nt - 32:nt],
                            op0=ALU.add, op1=ALU.mult), reads=[bth, bpa], writes=[b_u32])
            for j in range(3):
                pz, bpz = proj(O_ZB + 128 * j, c.hTb, c.b_hTb, nt)
                sz, bsz = s32.get()
                op("act", lambda e, pz=pz, sz=sz: e.activation(out=sz[:, 0:nt], in_=pz[:, 0:nt], func=AF.Silu),
                   reads=[bpz], writes=[bsz])
                c.szb.append((sz, bsz))
                c.hook()

        T_DVE = 6

        def phase_B_conv(c):
            nt = c.nt
            T = 0 if c.sample else c.tdve
            for j in range(3):
                part = None
                if T > 0:
                    part, bpart = s32.get()
                    op("dve", lambda e, part=part, j=j: e.tensor_scalar(out=part[:, 0:nt], in0=c.ub[:, j, 0:nt], scalar1=wcb[:, j, 0:1],
                                                                        scalar2=bcb[:, j:j + 1], op0=ALU.mult, op1=ALU.add),
                       reads=[c.b_ub[j], *smalls], writes=[bpart])
                    for k in range(1, T):
                        op("dve", lambda e, part=part, j=j, k=k: e.scalar_tensor_tensor(
                            out=part[:, 0:nt], in0=c.ub[:, j, k:k + nt], scalar=wcb[:, j, k:k + 1], in1=part[:, 0:nt],
                            op0=ALU.mult, op1=ALU.add), reads=[c.b_ub[j], *smalls, bpart], writes=[bpart])
                pc, bpc = banks.get()
                for k in range(T, 31):
                    if c.sample:
                        rhs = c.ub[:, j, :].rearrange("p (s w) -> p s w", w=38)[:, :, k:k + 8]
                        o = seqv(pc[:, 0:nt], 8)
                    else:
                        rhs = c.ub[:, j, k:k + nt]
                        o = pc[:, 0:nt]
                    op("pe", lambda e, o=o, rhs=rhs, j=j, k=k: e.matmul(o, lhsT=dgb[:, j, k, :], rhs=rhs,
                                                                       start=(k == T), stop=(k == 30)),
                       reads=[b_dgb[j], c.b_ub[j]], writes=[bpc] + ([c.gate] if (c.gate is not None and j == 0 and k == 30) else []),
                       inc=(k == 30))
                yq, byq = s16.get()
                if T > 0:
                    yy, byy = part, bpart
                    op("dve", lambda e, pc=pc, yy=yy: e.tensor_tensor(out=yy[:, 0:nt], in0=yy[:, 0:nt], in1=pc[:, 0:nt], op=ALU.add),
                       reads=[byy, bpc], writes=[byy])
                    op("act", lambda e, yy=yy, yq=yq: e.activation(out=yq[:, 0:nt], in_=yy[:, 0:nt], func=AF.Square),
                       reads=[byy], writes=[byq])
                else:
                    yy, byy = s32.get()
                    op("act", lambda e, pc=pc, yy=yy, j=j: e.activation(out=yy[:, 0:nt], in_=pc[:, 0:nt], func=AF.Identity,
                                                                         bias=bcb[:, j:j + 1]), reads=[bpc, *smalls], writes=[byy])
                    op("act", lambda e, pc=pc, yq=yq, j=j: e.activation(out=yq[:, 0:nt], in_=pc[:, 0:nt], func=AF.Square,
                                                                         bias=bcb[:, j:j + 1]), reads=[bpc, *smalls], writes=[byq])
                op("dve", lambda e, yq=yq, yy=yy: e.tensor_copy(out=yq[:, 512:512 + nt], in_=yy[:, 0:nt]),
                   reads=[byy], writes=[byq])
                c.y32.append((yy, byy))
                c.ysq.append((yq, byq))
                if not c.sample and not c.last:
                    op("pool", lambda e, j=j: e.tensor_copy(out=c.ub[:, j, 0:30], in_=c.ub[:, j, nt:nt + 30]),
                       reads=[], writes=[c.b_ub[j]])
                c.hook()

        def phase_LN(c):
            nt, ntile = c.nt, c.ntile
            pst, bpst = banks.get()
            for tt in range(ntile):
                for which in range(2):
                    for j in range(3):
                        yq, byq = c.ysq[j]
                        src = yq[:, 512 + tt * 128:512 + (tt + 1) * 128] if which == 0 else yq[:, tt * 128:(tt + 1) * 128]
                        op("pe", lambda e, src=src, tt=tt, which=which, j=j: e.matmul(
                            pst[:, 2 * tt + which:2 * tt + which + 1], lhsT=src, rhs=ones[:, 0:1], start=(j == 0), stop=(j == 2)),
                           reads=[byq, b_ones], writes=[bpst], inc=(j == 2 and which == 1 and tt == ntile - 1))
            pstv = pst[:, 0:2 * ntile].rearrange("p (t w) -> p t w", w=2)
            mean = lnt[:, 0:ntile, 0]
            msq = lnt[:, 0:ntile, 1]
            rstd = lnt[:, 0:ntile, 2]
            nb = lnt[:, 0:ntile, 3]
            op("dve", lambda e: e.tensor_scalar(out=mean, in0=pstv[:, :, 0], scalar1=1.0 / 384, scalar2=None, op0=ALU.mult),
               reads=[bpst], writes=[b_lnt])
            op("dve", lambda e: e.tensor_tensor(out=msq, in0=mean, in1=mean, op=ALU.mult), reads=[b_lnt], writes=[b_lnt])
            op("dve", lambda e: e.scalar_tensor_tensor(out=msq, in0=pstv[:, :, 1], scalar=1.0 / 384, in1=msq,
                                                       op0=ALU.mult, op1=ALU.subtract), reads=[bpst, b_lnt], writes=[b_lnt])
            op("pool", lambda e: e.tensor_scalar(out=rstd, in0=msq, scalar1=1.0, scalar2=EPS, op0=ALU.mult, op1=ALU.add),
               reads=[b_lnt], writes=[b_lnt])
            op("pool", lambda e: e.tensor_tensor(out=rstd, in0=rstd, in1=nhalf[:, 0:ntile], op=ALU.pow),
               reads=[b_lnt, b_nhalf], writes=[b_lnt])
            op("dve", lambda e: e.scalar_tensor_tensor(out=nb, in0=mean, scalar=-1.0, in1=rstd, op0=ALU.mult, op1=ALU.mult),
               reads=[b_lnt], writes=[b_lnt])

        def phase_LN_bcast(c):
            nt, ntile = c.nt, c.ntile
            prb, bprb = banks.get()
            pmb, bpmb = banks.get()
            c.prb, c.bprb, c.pmb, c.bpmb = prb, bprb, pmb, bpmb
            for tt in range(ntile):
                op("pe", lambda e, tt=tt: e.matmul(prb[:, tt * 128:(tt + 1) * 128], lhsT=lnt[:, tt, 2:3].to_broadcast([128, 128]),
                                                   rhs=idf[:], start=True, stop=True),
                   reads=[b_lnt, b_idf], writes=[bprb], inc=(tt == ntile - 1))
            for tt in range(ntile):
                op("pe", lambda e, tt=tt: e.matmul(pmb[:, tt * 128:(tt + 1) * 128], lhsT=lnt[:, tt, 3:4].to_broadcast([128, 128]),
                                                   rhs=idf[:], start=True, stop=True),
                   reads=[b_lnt, b_idf], writes=[bpmb], inc=(tt == ntile - 1))

        def phase_LN_elem(c):
            nt = c.nt
            prb, bprb, pmb, bpmb = c.prb, c.bprb, c.pmb, c.bpmb
            for j in range(3):
                yy, byy = c.y32[j]
                sz, bsz = c.szb[j]
                op("dve", lambda e, yy=yy: e.tensor_tensor(out=yy[:, 0:nt], in0=yy[:, 0:nt], in1=prb[:, 0:nt], op=ALU.mult),
                   reads=[byy, bprb], writes=[byy])
                op("dve", lambda e, yy=yy: e.tensor_tensor(out=yy[:, 0:nt], in0=yy[:, 0:nt], in1=pmb[:, 0:nt], op=ALU.add),
                   reads=[byy, bpmb], writes=[byy])
                op("act", lambda e, yy=yy, j=j: e.activation(out=yy[:, 0:nt], in_=yy[:, 0:nt], func=AF.Silu,
                                                              scale=lng[:, j:j + 1], bias=lnb[:, j:j + 1]),
                   reads=[byy, *smalls], writes=[byy])
                op("dve", lambda e, yy=yy, sz=sz, j=j: e.tensor_tensor(out=c.oT[:, 3 + j, 0:nt], in0=yy[:, 0:nt], in1=sz[:, 0:nt],
                                                                       op=ALU.mult), reads=[byy, bsz], writes=[c.b_oT[3 + j]])

        def phase_A(c):
            nt, S, L = c.nt, c.S, c.L
            W = L + 2
            for j in range(3):
                pcg, bpcg = proj(O_CG + 128 * j, c.hTb, c.b_hTb, nt)
                cgs, bcgs = s32.get()
                op("act", lambda e, pcg=pcg, cgs=cgs: e.activation(out=cgs[:, 0:nt], in_=pcg[:, 0:nt], func=AF.Copy),
                   reads=[bpcg], writes=[bcgs])
                pv_, bpv_ = proj(O_V + 128 * j, c.hTb, c.b_hTb, nt)
                cb, bcb_ = cvp.get()
                cbv = cb[:, 0:S * W].rearrange("p (s w) -> p s w", w=W)
                op("dve", lambda e, cbv=cbv, j=j: e.tensor_copy(out=cbv[:, :, 0:2],
                                                                in_=c.cvh[:, j, 0:2 * S].rearrange("p (s w) -> p s w", w=2)),
                   reads=[c.b_cvh], writes=[bcb_])
                op("dve", lambda e, cbv=cbv, cgs=cgs, pv_=pv_: e.tensor_tensor(out=cbv[:, :, 2:W], in0=seqv(cgs[:, 0:nt], L),
                                                                              in1=seqv(pv_[:, 0:nt], L), op=ALU.mult),
                   reads=[bcgs, bpv_], writes=[bcb_])
                op("dve", lambda e, cbv=cbv, j=j: e.tensor_copy(out=c.cvh[:, j, 0:2 * S].rearrange("p (s w) -> p s w", w=2),
                                                                in_=cbv[:, :, L:W]), reads=[bcb_], writes=[c.b_cvh])
                acc, bacc = s32.get()
                accv = seqv(acc[:, 0:nt], L)
                op("dve", lambda e, accv=accv, cbv=cbv, j=j: e.tensor_scalar(out=accv, in0=cbv[:, :, 0:L], scalar1=wca[:, j, 0:1],
                                                                             scalar2=bca[:, j:j + 1], op0=ALU.mult, op1=ALU.add),
                   reads=[bcb_, *smalls], writes=[bacc])
                for k in (1, 2):
                    op("dve", lambda e, accv=accv, cbv=cbv, j=j, k=k: e.scalar_tensor_tensor(
                        out=accv, in0=cbv[:, :, k:k + L], scalar=wca[:, j, k:k + 1], in1=accv, op0=ALU.mult, op1=ALU.add),
                       reads=[bcb_, *smalls, bacc], writes=[bacc])
                pz, bpz = proj(O_ZA + 128 * j, c.hTb, c.b_hTb, nt)
                sz, bsz = s32.get()
                op("act", lambda e, pz=pz, sz=sz: e.activation(out=sz[:, 0:nt], in_=pz[:, 0:nt], func=AF.Silu),
                   reads=[bpz], writes=[bsz])
                pbg, bpbg = proj(O_BG + 128 * j, c.hTb, c.b_hTb, nt)
                op("dve", lambda e, sz=sz, pbg=pbg: e.tensor_tensor(out=sz[:, 0:nt], in0=sz[:, 0:nt], in1=pbg[:, 0:nt], op=ALU.mult),
                   reads=[bsz, bpbg], writes=[bsz])
                op("dve", lambda e, sz=sz, acc=acc, j=j: e.tensor_tensor(out=c.oT[:, j, 0:nt], in0=acc[:, 0:nt], in1=sz[:, 0:nt],
                                                                         op=ALU.mult), reads=[bsz, bacc], writes=[c.b_oT[j]])
                c.hook()

        def phase_X_proj(c):
            nt = c.nt
            for jx in range(2):
                pz, bpz = proj(O_ZX + 128 * jx, c.hTb, c.b_hTb, nt)
                if c.sample:
                    op("act", lambda e, pz=pz, jx=jx: e.activation(out=szx_s[:, jx, 0:nt], in_=pz[:, 0:nt], func=AF.Silu),
                       reads=[bpz], writes=[b_szx_s])
                else:
                    sz, bsz = s32.get()
                    op("act", lambda e, pz=pz, sz=sz: e.activation(out=sz[:, 0:nt], in_=pz[:, 0:nt], func=AF.Silu),
                       reads=[bpz], writes=[bsz])
                    c.szx.append((sz, bsz))
            if not c.sample:
                op("act", lambda e: e.activation(out=expw[:, 0:1], in_=nhalf[:, 0:1], func=AF.Exp), reads=[b_nhalf], writes=[b_expw])
            for jx in range(2):
                pq, bpq = proj(O_Q + 128 * jx, c.hTb, c.b_hTb, nt)
                op("act", lambda e, pq=pq, jx=jx: e.activation(out=c.qT[:, jx, 0:nt], in_=pq[:, 0:nt], func=AF.Copy),
                   reads=[bpq], writes=[c.b_qT[jx]])

        def phase_LN_apply(c):
            phase_LN_bcast(c)
            phase_LN_elem(c)

        def phase_X_attn(c, KV, pairs=(0, 1)):
            nt = c.nt
            KT_, bKT, V_, bV = KV
            pO, pD, pts = {}, {}, {}

            def S(h):
                jx = h // 2
                pt, bpt = s16.get()
                pts[h] = (pt, bpt)
                for mc in range(2):
                    ps_, bps = banks.get()
                    op("pe", lambda e, ps_=ps_, jx=jx, h=h, mc=mc: e.matmul(
                        ps_[:, 0:nt], lhsT=KT_[:, h, mc * 128:(mc + 1) * 128], rhs=c.qT[:, jx, 0:nt],
                        start=True, stop=True), reads=[bKT, c.b_qT[jx]], writes=[bps])
                    op("act", lambda e, ps_=ps_, pt=pt, mc=mc: e.activation(out=pt[:, mc * 512:mc * 512 + nt], in_=ps_[:, 0:nt],
                                                                             func=AF.Exp), reads=[bps], writes=[bpt])

            def PVD(h):
                jx = h // 2
                pt, bpt = pts.pop(h)
                first, last = (h % 2 == 0), (h % 2 == 1)
                if first:
                    pO[jx] = banks.get()
                    pD[jx] = banks.get()
                for mc in range(2):
                    op("pe", lambda e, h=h, jx=jx, mc=mc, pt=pt, first=first, last=last: e.matmul(
                        pO[jx][0][:, 0:nt], lhsT=V_[:, mc, h, :], rhs=pt[:, mc * 512:mc * 512 + nt],
                        start=(first and mc == 0), stop=(last and mc == 1)), reads=[bV, bpt], writes=[pO[jx][1]], inc=(mc == 1))
                for mc in range(2):
                    op("pe", lambda e, h=h, jx=jx, mc=mc, pt=pt, first=first, last=last: e.matmul(
                        pD[jx][0][:, 0:nt], lhsT=onesz[:, h % 2, :], rhs=pt[:, mc * 512:mc * 512 + nt],
                        start=(first and mc == 0), stop=(last and mc == 1)), reads=[b_onesz, bpt], writes=[pD[jx][1]], inc=(mc == 1))
                if not last:
                    return
                rec, brec = s32.get()
                sz, bsz = c.szx[jx]
                op("dve", lambda e, rec=rec, jx=jx: e.reciprocal(out=rec[:, 0:nt], in_=pD[jx][0][:, 0:nt]),
                   reads=[pD[jx][1]], writes=[brec])
                op("dve", lambda e, rec=rec, jx=jx: e.tensor_tensor(out=rec[:, 0:nt], in0=rec[:, 0:nt], in1=pO[jx][0][:, 0:nt],
                                                                    op=ALU.mult), reads=[brec, pO[jx][1]], writes=[brec])
                op("dve", lambda e, rec=rec, sz=sz, jx=jx: e.tensor_tensor(out=c.oT[:, 6 + jx, 0:nt], in0=rec[:, 0:nt], in1=sz[:, 0:nt],
                                                                           op=ALU.mult), reads=[brec, bsz], writes=[c.b_oT[6 + jx]])

            heads = [h for h in range(4) if h // 2 in pairs]
            S(heads[0])
            for i, h in enumerate(heads):
                if i + 1 < len(heads):
                    S(heads[i + 1])
                PVD(h)

        def sample_seq_load(s):
            st, bst = xpool.get()
            stk = st[:, 0:512].rearrange("p (m f) -> p m f", f=256)
            stv = st[:, 512:1024].rearrange("p (m f) -> p m f", f=256)
            dma("sp", stk, ck[s].rearrange("(m p) f -> p m f", p=128), writes=[bst])
            dma("sp", stv, cv[s].rearrange("(m p) f -> p m f", p=128), writes=[bst])
            kvb, bkvb = s16.get()
            op("dve", lambda e: e.tensor_copy(out=kvb[:], in_=st[:]), reads=[bst], writes=[bkvb])
            return kvb, bkvb

        seq_loaded = {}
        seq_started = set()

        def sample_attn_stages(s):
            if s not in seq_loaded:
                seq_loaded[s] = sample_seq_load(s)
            kvb, bkvb = seq_loaded.pop(s)
            for s2 in (s + 1, s + 2):
                if s2 < 16 and s2 not in seq_loaded and s2 not in seq_started:
                    seq_loaded[s2] = sample_seq_load(s2)
            seq_started.add(s)
            ptk, bptk = banks.get()
            ptkv = ptk[:].bitcast(BF16).rearrange("p (j m) -> p j m", m=256)
            for mc in range(2):
                for jx in range(2):
                    op("pe", lambda e, mc=mc, jx=jx: e.transpose(
                        out=ptkv[:, jx, mc * 128:(mc + 1) * 128], in_=kvb[:, mc * 256 + jx * 128:mc * 256 + (jx + 1) * 128],
                        identity=idb[:]), reads=[bkvb, b_idb], writes=[bptk], inc=(mc == 1 and jx == 1))
            kts, bkts = s16.get()
            op("dve", lambda e: e.tensor_copy(out=kts[:, 0:512].rearrange("p (j m) -> p j m", m=256), in_=ptkv[:, 0:2, :]),
               reads=[bptk], writes=[bkts])
            ktsv = kts[:, 0:512].rearrange("p (j m) -> p j m", m=256)
            yield
            pt = kts[:, 512:576]
            bpt_ = Buf("pt_s")
            for par in range(2):
                ps_, bps = banks.get()
                for jx in range(2):
                    po = 64 * par
                    for mc in range(2):
                        cb = (jx * 2 + mc) * 8
                        op("pe", lambda e, ps_=ps_, jx=jx, po=po, mc=mc, cb=cb: e.matmul(
                            ps_[:, cb:cb + 8], lhsT=ktsv[po:po + 64, jx, mc * 128:(mc + 1) * 128],
                            rhs=qTs[po:po + 64, jx, s * 8:(s + 1) * 8], start=True, stop=True),
                           reads=[bkts, b_qTs[jx]], writes=[bps], inc=(jx == 1 and mc == 1))
                op("act", lambda e, ps_=ps_, par=par: e.activation(out=pt[:, par * 32:(par + 1) * 32], in_=ps_[:, 0:32], func=AF.Exp),
                   reads=[bps], writes=[bpt_])
            yield
            pod, bpod = banks.get()
            for h in range(4):
                jx, po = h // 2, 64 * (h % 2)
                for kind in range(2):
                    for mc in range(2):
                        c0 = (h % 2) * 32 + ((h // 2) * 2 + mc) * 8
                        lhsT = kvb[:, 512 + mc * 256 + h * 64:512 + mc * 256 + (h + 1) * 64] if kind == 0 else ones[:, 0:64]
                        oc = kind * 16 + jx * 8
                        op("pe", lambda e, po=po, mc=mc, c0=c0, lhsT=lhsT, oc=oc: e.matmul(
                            pod[po:po + 64, oc:oc + 8], lhsT=lhsT, rhs=pt[:, c0:c0 + 8], start=(mc == 0), stop=(mc == 1)),
                           reads=[bkvb, bkts, bpt_, b_ones], writes=[bpod], inc=(h == 3 and kind == 1 and mc == 1))
            op("dve", lambda e: e.tensor_copy(out=OD_s[:, :, s * 8:(s + 1) * 8], in_=pod[:, 0:32].rearrange("p (a w) -> p a w", w=8)),
               reads=[bpod], writes=[b_OD_s])
            yield

        class Weaver:
            def __init__(self, seqs):
                self.seqs = list(seqs)
                self.act = []
                self.rr = 0

            def _fill(self):
                while len(self.act) < 2 and self.seqs:
                    self.act.append(sample_attn_stages(self.seqs.pop(0)))

            def __call__(self):
                self._fill()
                while self.act:
                    self.rr = (self.rr + 1) % len(self.act)
                    g = self.act[self.rr]
                    try:
                        next(g)
                        return
                    except StopIteration:
                        self.act.remove(g)
                        self._fill()

            def drain(self):
                self._fill()
                while self.act:
                    self()

        def sample_attn_finish():
            for jx in range(2):
                rec, brec = s32.get()
                op("dve", lambda e, rec=rec, jx=jx: e.reciprocal(out=rec[:, 0:128], in_=OD_s[:, 2 + jx, :]),
                   reads=[b_OD_s], writes=[brec])
                op("dve", lambda e, rec=rec, jx=jx: e.tensor_tensor(out=rec[:, 0:128], in0=rec[:, 0:128], in1=OD_s[:, jx, :],
                                                                    op=ALU.mult), reads=[brec, b_OD_s], writes=[brec])
                op("dve", lambda e, rec=rec, jx=jx: e.tensor_tensor(out=outT_s[:, 6 + jx, :], in0=rec[:, 0:128], in1=szx_s[:, jx, :],
                                                                    op=ALU.mult), reads=[brec, b_szx_s], writes=[b_outT_s[6 + jx]])


        slot = [0]

        def wout_block(nt, xrows, yrows, oT, b_oT):
            ntile = nt // 128
            pending = []

            def finish(xr, bxr, sl, tt):
                op("dve", lambda e, xr=xr, sl=sl: e.scalar_tensor_tensor(out=xr[:], in0=xr[:], scalar=rs2[:, sl:sl + 1], in1=gfin[:],
                                                                         op0=ALU.mult, op1=ALU.mult),
                   reads=[bxr, b_rs2[sl], b_gfin], writes=[bxr])
                dma("pool", yrows(tt), xr[:], reads=[bxr], store=True)

            tiles = {}
            for tt0 in range(0, ntile, 2):
                grp = [tt for tt in (tt0, tt0 + 1) if tt < ntile]
                for tt in grp:
                    xr, bxr = xpool.get()
                    dma("sp", xr[:], xrows(tt), writes=[bxr])
                    tiles[tt] = (xr, bxr, [banks.get(), banks.get()])
                for kcs in (range(0, 7), (7,)):
                    for tt in grp:
                        for half in range(2):
                            pb, bpb = tiles[tt][2][half]
                            for kc in kcs:
                                op("pe", lambda e, pb=pb, kc=kc, tt=tt, half=half: e.matmul(
                                    pb[:, 0:512], lhsT=oT[:, kc, tt * 128:(tt + 1) * 128], rhs=wout[:, kc, half * 512:(half + 1) * 512],
                                    start=(kc == 0), stop=(kc == 7)), reads=[b_oT[kc], b_wout[kc]], writes=[bpb], inc=(kc in (6, 7)))
            for tt in range(ntile):
                xr, bxr, bk = tiles[tt]
                sl = slot[0] % 8
                slot[0] += 1
                for half in range(2):
                    pb, bpb = bk[half]
                    op("dve", lambda e, pb=pb, xr=xr, half=half: e.tensor_tensor(
                        out=xr[:, half * 512:(half + 1) * 512], in0=xr[:, half * 512:(half + 1) * 512], in1=pb[:, 0:512], op=ALU.add),
                       reads=[bxr, bpb], writes=[bxr])
                jk, bjk = s16.get()
                op("act", lambda e, xr=xr, sl=sl, jk=jk: e.activation(out=jk[:], in_=xr[:], func=AF.Square, accum_out=ss2[:, sl:sl + 1]),
                   reads=[bxr], writes=[bjk, b_ss2[sl]])
                op("pool", lambda e, sl=sl: e.tensor_scalar(out=rs2[:, sl:sl + 1], in0=ss2[:, sl:sl + 1], scalar1=1.0 / 1024,
                                                            scalar2=EPS, op0=ALU.mult, op1=ALU.add),
                   reads=[b_ss2[sl]], writes=[b_rs2[sl]])
                op("pool", lambda e, sl=sl: e.tensor_tensor(out=rs2[:, sl:sl + 1], in0=rs2[:, sl:sl + 1], in1=nhalf[:, 0:1], op=ALU.pow),
                   reads=[b_rs2[sl], b_nhalf], writes=[b_rs2[sl]])
                if pending:
                    finish(*pending.pop(0))
                pending.append((xr, bxr, sl, tt))
            while pending:
                finish(*pending.pop(0))

        def _program():
            op("dve", lambda e: e.memset(cvh[:], 0.0), writes=[b_cvh])
            for j in range(3):
                op("pool", lambda e, j=j: e.memset(ubuf[:, j, 0:30], 0.0), writes=[b_ub[j]])
            issue_win((0, 1, 2))
            xh0 = prep_a(lambda tt: xp[tt * 128:(tt + 1) * 128, :], 4)
            load_small_consts()
            issue_win((3, 4))
            scale_win((0, 1, 2))
            def build_diag(j):
                op("dve", lambda e, j=j: e.tensor_tensor(out=dgb[:, j, :, :], in0=idf[:].unsqueeze(1).to_broadcast([128, 31, 128]),
                                                         in1=wcb[:, j, :].unsqueeze(2).to_broadcast([128, 31, 128]), op=ALU.mult),
                   reads=[b_idf, *smalls], writes=[b_dgb[j]])
            build_diag(0)
            prep_b(xh0, hT[1], b_hT[1], 0)
            op("pool", lambda e: e.memset(KTz[:], 0.0), writes=[b_KTp])
            op("pool", lambda e: e.memset(Vz[:], 0.0), writes=[b_Vp])
            op("pool", lambda e: e.memset(onesz[:], 0.0), writes=[b_onesz])
            for p_ in range(2):
                op("pool", lambda e, p_=p_: e.memset(onesz[:, p_, 64 * p_:64 * p_ + 64], 1.0), writes=[b_onesz])
            mem_state = {}
            def sample_states():
                sst, bsst = s32.get()
                dma("sp", sst[0:32, 0:384], sca[:, :], writes=[bsst])
                pb, bpb = banks.get()
                for j in range(3):
                    op("pe", lambda e, pb=pb, j=j: e.transpose(out=pb[:, j * 32:(j + 1) * 32], in_=sst[0:32, j * 128:(j + 1) * 128],
                                                               identity=idf[0:32, 0:32]), reads=[bsst, b_idf], writes=[bpb], inc=(j == 2))
                op("dve", lambda e, pb=pb: e.tensor_copy(out=cvh_s[:, :, :], in_=pb[:, 0:96].rearrange("p (j w) -> p j w", w=32)),
                   reads=[bpb], writes=[b_cvh_s])
                for g in range(4):
                    st, bst = s32.get()
                    dma("sp", st[0:120, 0:384], scb[4 * g:4 * g + 4].rearrange("s l c -> (s l) c"), writes=[bst])
                    pb, bpb = banks.get()
                    for j in range(3):
                        op("pe", lambda e, pb=pb, st=st, j=j: e.transpose(out=pb[:, j * 120:(j + 1) * 120], in_=st[0:120, j * 128:(j + 1) * 128],
                                                                          identity=idf[0:120, 0:120]), reads=[bst, b_idf], writes=[bpb],
                           inc=(j == 2))
                    for j in range(3):
                        dst = ubuf_s[:, j, :].rearrange("p (s w) -> p s w", w=38)[:, 4 * g:4 * g + 4, 0:30]
                        op("dve", lambda e, pb=pb, dst=dst, j=j: e.tensor_copy(
                            out=dst, in_=pb[:, j * 120:(j + 1) * 120].rearrange("p (s l) -> p s l", l=30)), reads=[bpb], writes=[b_ub_s[j]])
                dma("pool", o_sb[:, 0:22, :], scb[:, 8:30, :], store=True)


            def prompt_state_a():
                pb, bpb = banks.get()
                for j in range(3):
                    op("pe", lambda e, pb=pb, j=j: e.transpose(out=pb[0:2, j * 128:(j + 1) * 128], in_=cvh[:, j, 0:2], identity=idf[:]),
                       reads=[b_cvh, b_idf], writes=[bpb], inc=(j == 2))
                so, bso = s32.get()
                op("act", lambda e, pb=pb, so=so: e.activation(out=so[0:2, 0:384], in_=pb[0:2, 0:384], func=AF.Copy), reads=[bpb], writes=[bso])
                dma("pool", o_pa[:, :], so[0:2, 0:384], reads=[bso], store=True)

            def prompt_state_b():
                pb, bpb = banks.get()
                for j in range(3):
                    op("pe", lambda e, pb=pb, j=j: e.transpose(out=pb[0:32, j * 128:(j + 1) * 128], in_=u32[:, j, 0:32], identity=idf[:]),
                       reads=[b_u32, b_idf], writes=[bpb], inc=(j == 2))
                so2, bso2 = s32.get()
                op("act", lambda e, pb=pb, so2=so2: e.activation(out=so2[0:32, 0:384], in_=pb[0:32, 0:384], func=AF.Copy), reads=[bpb], writes=[bso2])
                dma("pool", o_pb[:, :], so2[2:32, 0:384], reads=[bso2], store=True)


            def prompt_block(b, wv):
                cur = (b + 1) % 2
                c = new_ctx(hT[cur], b_hT[cur], 512, False, b == 3, outT, b_outT, qT, b_qT)
                st = {"xh": None, "n": 0}
                nxt_rows = lambda tt, b=b: xp[(b + 1) * 512 + tt * 128:(b + 1) * 512 + (tt + 1) * 128, :]

                def hook_b():
                    wv()
                    if st["n"] == 1 and b < 3:
                        st["xh"] = prep_a(nxt_rows, 4)
                    st["n"] += 1
                    wv()
                c.hook = hook_b
                phase_B_proj(c)
                if b == 0:
                    build_diag(1)
                    build_diag(2)
                    scale_win((3, 4))
                if b > 0 and b < 3:
                    prep_b(st["xh"], hT[b % 2], b_hT[b % 2], 0)

                def hook_w():
                    wv()
                    wv()
                c.hook = hook_w
                if b == 0:
                    c.gate = b_gate
                    c.tdve = 0
                phase_B_conv(c)
                if b == 0:
                    issue_win((5,), gate=[b_gate])
                    load_wout(gate=[b_gate])
                    mem_gate = [c.szb[0][1]]
                    mem_xts = prep_load(lambda tt: mem[tt * 128:(tt + 1) * 128, :], 2, gate=mem_gate)
                    issue_wkv(gate=mem_gate)
                if b == 3:
                    prompt_state_b()
                st["a"] = 0

                def hook_a():
                    if st["a"] == 0:
                        phase_LN(c)
                    elif st["a"] == 1:
                        wv()
                        wv()
                    else:
                        wv()
                        wv()
                        phase_LN_bcast(c)
                        phase_LN_elem(c)
                    st["a"] += 1
                c.hook = hook_a
                phase_A(c)
                if b == 3:
                    prompt_state_a()
                if b == 0:
                    mem_state["xh"] = prep_a(lambda tt: mem[tt * 128:(tt + 1) * 128, :], 2, mem_xts)
                    scale_wkv()
                    sample_states()
                    mem_kv()
                    prep_b(st["xh"], hT[b % 2], b_hT[b % 2], 0)
                    scale_win((5,))
                phase_X_proj(c)
                wv()
                if b == 3:
                    wv.drain()
                    sample_attn_finish()
                phase_X_attn(c, (KTz, b_KTp, Vz, b_Vp))
                if b == 0:
                    xs_state["xh"] = prep_a(lambda tt: xs[:, :], 1)
                    hs_t, hs_b = s16.get()
                    xs_state["hT"] = (hs_t[:, 0:1024].rearrange("p (k c) -> p k c", c=128), hs_b)
                    prep_b(xs_state["xh"], xs_state["hT"][0], hs_b, 0)
                if b == 3:
                    wout_block(128, lambda tt: xs[:, :], lambda tt: ys[:, :], outT_s, b_outT_s)
                wout_block(512, lambda tt, b=b: xp[b * 512 + tt * 128:b * 512 + (tt + 1) * 128, :],
                           lambda tt, b=b: yp[b * 512 + tt * 128:b * 512 + (tt + 1) * 128, :], outT, b_outT)

            xs_state = {}

            def mem_kv():
                mTt = [s16.get(), s16.get()]
                mTv = [t[0][:, 0:1024].rearrange("p (k c) -> p k c", c=256) for t in mTt]
                for tt, (xh, bxh) in enumerate(mem_state["xh"]):
                    pbm, bpbm = banks.get()
                    pbv = pbm[:].bitcast(BF16).rearrange("p (k c) -> p k c", c=128)
                    for kc in range(8):
                        op("pe", lambda e, pbv=pbv, xh=xh, kc=kc: e.transpose(out=pbv[:, kc, :], in_=xh[:, kc * 128:(kc + 1) * 128],
                                                                             identity=idb[:]),
                           reads=[bxh, b_idb], writes=[bpbm], inc=(kc == 7))
                    for hh in range(2):
                        op("act", lambda e, pbv=pbv, hh=hh, tt=tt: e.activation(out=mTv[hh][:, :, tt * 128:(tt + 1) * 128],
                                                                                in_=pbv[:, 4 * hh:4 * hh + 4, :], func=AF.Copy),
                           reads=[bpbm], writes=[mTt[hh][1]])

                def mT_of(kc):
                    return mTv[kc // 4][:, kc % 4, :], mTt[kc // 4][1]
                for mt in range(2):
                    pb, bpb = banks.get()
                    for kc in range(8):
                        op("pe", lambda e, pb=pb, kc=kc, mt=mt: e.matmul(pb[:, 0:512], lhsT=mT_of(kc)[0][:, mt * 128:(mt + 1) * 128], rhs=wkv[:, kc, :],
                                                                         start=(kc == 0), stop=(kc == 7)), reads=[mT_of(kc)[1]] + b_wkv, writes=[bpb],
                           inc=(kc == 7))
                    kvf, bkvf = s32.get()
                    op("act", lambda e, pb=pb, kvf=kvf: e.activation(out=kvf[:], in_=pb[:, 0:512], func=AF.Copy), reads=[bpb], writes=[bkvf])
                    for h in range(4):
                        po = 64 * (h % 2)
                        op("dve", lambda e, pb=pb, mt=mt, h=h, po=po: e.tensor_copy(out=Vz[:, mt, h, po:po + 64],
                                                                                    in_=pb[:, 256 + h * 64:256 + (h + 1) * 64]),
                           reads=[bpb], writes=[b_Vp])
                    dma("pool", o_pk[mt * 128:(mt + 1) * 128, :], kvf[:, 0:256], reads=[bkvf], store=True)
                    dma("pool", o_pv[mt * 128:(mt + 1) * 128, :], kvf[:, 256:512], reads=[bkvf], store=True)
                for jx in range(2):
                    pb, bpb = banks.get()
                    for kc in range(8):
                        op("pe", lambda e, pb=pb, kc=kc, jx=jx: e.matmul(pb[:, 0:256], lhsT=wkv[:, kc, jx * 128:(jx + 1) * 128], rhs=mT_of(kc)[0][:, 0:256],
                                                                         start=(kc == 0), stop=(kc == 7)), reads=[mT_of(kc)[1]] + b_wkv, writes=[bpb],
                           inc=(kc == 7))
                    for hh in range(2):
                        po = 64 * hh
                        op("act", lambda e, pb=pb, jx=jx, hh=hh, po=po: e.activation(out=KTz[po:po + 64, 2 * jx + hh, :], in_=pb[po:po + 64, 0:256],
                                                                                     func=AF.Copy), reads=[bpb], writes=[b_KTp])


            prompt_block(0, lambda: None)
            for s_ in range(2):
                seq_loaded[s_] = sample_seq_load(s_)
            if stage < 6:
                return
            hT_s, b_hT_s = xs_state["hT"]
            cs = new_ctx(hT_s, b_hT_s, 128, True, False, outT_s, b_outT_s, qTs, b_qTs, ubuf_s, b_ub_s, cvh_s, b_cvh_s)
            phase_B_proj(cs)
            phase_A(cs)
            phase_B_conv(cs)
            phase_X_proj(cs)
            phase_LN(cs)
            phase_LN_bcast(cs)
            phase_LN_elem(cs)
            pb, bpb = banks.get()
            for j in range(3):
                op("pe", lambda e, pb=pb, j=j: e.transpose(out=pb[0:32, j * 128:(j + 1) * 128], in_=cvh_s[:, j, 0:32], identity=idf[:]),
                   reads=[b_cvh_s, b_idf], writes=[bpb], inc=(j == 2))
            so, bso = s32.get()
            op("act", lambda e, pb=pb, so=so: e.activation(out=so[0:32, 0:384], in_=pb[0:32, 0:384], func=AF.Copy), reads=[bpb], writes=[bso])
            dma("pool", o_sa[:, :], so[0:32, 0:384], reads=[bso], store=True)
            pb, bpb = banks.get()
            for j in range(3):
                op("pe", lambda e, pb=pb, j=j: e.transpose(out=pb[:, j * 128:(j + 1) * 128], in_=u32[:, j, 0:128], identity=idf[:]),
                   reads=[b_u32, b_idf], writes=[bpb], inc=(j == 2))
            so2, bso2 = s32.get()
            op("act", lambda e, pb=pb, so2=so2: e.activation(out=so2[:, 0:384], in_=pb[:, 0:384], func=AF.Copy), reads=[bpb], writes=[bso2])
            for s_ in range(16):
                dma("sp", o_sb[s_, 22:30, :], so2[s_ * 8:(s_ + 1) * 8, 0:384], reads=[bso2], store=True)
            if stage < 8:
                return
            wv = Weaver(range(16))
            for b in range(1, 4):
                prompt_block(b, wv)
        _program()
        if life is None:
            return {k: p.lifetimes() for k, p in pools.items()}
        fw.finish()
    return nc


_NC_CACHE = {}


def kernel(x_prompt, x_sample, state_conv_a, state_conv_b, cache_mem_k, cache_mem_v, mem_prompt,
           g_norm, w_in, w_conv_a, b_conv_a, w_conv_b, b_conv_b, ln_g, ln_b, w_out, g_mem,
           w_mem_k, w_mem_v, g_final):
    f = lambda a: np.ascontiguousarray(np.asarray(a, dtype=np.float32))
    x_prompt, x_sample = f(x_prompt), f(x_sample)
    state_conv_a, state_conv_b = f(state_conv_a), f(state_conv_b)
    cache_mem_k, cache_mem_v, mem_prompt = f(cache_mem_k), f(cache_mem_v), f(mem_prompt)
    if "nc" not in _NC_CACHE:
        _NC_CACHE["nc"] = build_nc()
    nc = _NC_CACHE["nc"]
    shared = dict(
        g_norm=f(g_norm)[0], w_in=f(w_in)[0], w_conv_a=f(w_conv_a)[0], b_conv_a=f(b_conv_a)[0],
        w_conv_b=f(w_conv_b)[0], b_conv_b=f(b_conv_b)[0], ln_g=f(ln_g)[0], ln_b=f(ln_b)[0],
        w_out=f(w_out)[0], g_mem=f(g_mem)[0], w_mem_k=f(w_mem_k)[0], w_mem_v=f(w_mem_v)[0],
        g_final=f(g_final), ident=np.eye(128, dtype=np.float32),
    )
    in_maps = []
    for c in range(NCORES):
        s0, s1 = 16 * c, 16 * (c + 1)
        m = dict(shared)
        m["xp"] = x_prompt[c]
        m["xs"] = np.ascontiguousarray(x_sample[s0:s1].reshape(128, 1024))
        m["sca"] = np.ascontiguousarray(state_conv_a[0, s0:s1].reshape(32, 384))
        m["scb"] = np.ascontiguousarray(state_conv_b[0, s0:s1])
        m["ck"] = np.ascontiguousarray(cache_mem_k[0, s0:s1].reshape(16, 256, 256))
        m["cv"] = np.ascontiguousarray(cache_mem_v[0, s0:s1].reshape(16, 256, 256))
        m["mem"] = mem_prompt[c]
        in_maps.append(m)
    res = run_bass_kernel_spmd(nc, in_maps, core_ids=list(range(NCORES)))
    R = res.results
    y_prompt = np.stack([R[c]["yp"] for c in range(NCORES)], 0)
    y_sample = np.concatenate([R[c]["ys"].reshape(16, 8, 1024) for c in range(NCORES)], 0)
    pa = np.stack([R[c]["o_pa"] for c in range(NCORES)], 0)[None]
    pb = np.stack([R[c]["o_pb"] for c in range(NCORES)], 0)[None]
    pk = np.stack([R[c]["o_pk"].reshape(256, 4, 64) for c in range(NCORES)], 0)[None]
    pv = np.stack([R[c]["o_pv"].reshape(256, 4, 64) for c in range(NCORES)], 0)[None]
    sa = np.concatenate([R[c]["o_sa"].reshape(16, 2, 384) for c in range(NCORES)], 0)[None]
    sb = np.concatenate([R[c]["o_sb"] for c in range(NCORES)], 0)[None]
    return (y_prompt, y_sample, pa, pb, pk, pv, sa, sb)
```

```python
import os
import numpy as np
import concourse.bass as bass
import concourse.mybir as mybir
from concourse.bass_utils import run_bass_kernel_spmd
from contextlib import ExitStack

F32 = mybir.dt.float32
BF16 = mybir.dt.bfloat16
AF = mybir.ActivationFunctionType
ALU = mybir.AluOpType

ENGS = ("pe", "act", "dve", "pool", "sp")
EPS = 1e-6
NCORES = 8


class Buf:
    __slots__ = ("name", "w", "r", "rel", "psum")

    def __init__(self, name, psum=False):
        self.name = name
        self.w = None
        self.r = []
        self.rel = 0
        self.psum = psum


class FW:
    NDMA = 48

    def __init__(self, nc, es):
        self.nc = nc
        self.es = es
        self.q = {e: [] for e in ENGS}
        self.cnt = {e: 0 for e in ENGS}
        self.sem = {}
        for e in ("pe", "act", "dve", "pool"):
            self.sem[e] = es.enter_context(nc.semaphore("s_" + e))
        self.dsem = [es.enter_context(nc.semaphore("d%d" % i)) for i in range(self.NDMA)]
        self.dval = [0] * self.NDMA
        self.dnext = {"sw": 0, "hw": 0}
        self.dbase = {"sw": (0, self.NDMA // 2), "hw": (self.NDMA // 2, self.NDMA - self.NDMA // 2)}
        self.waited = {e: {} for e in ENGS}
        self.seq = 0
        self.store_tickets = []

    def _semof(self, key):
        return self.sem[key] if isinstance(key, str) else self.dsem[key[1]]

    def _need(self, eng, tickets):
        best = {}
        for t in tickets:
            if t is None:
                continue
            k, v = t
            if k == eng and eng == "pe":
                continue
            if best.get(k, 0) < v:
                best[k] = v
        out = []
        for k, v in best.items():
            if self.waited[eng].get(k, 0) >= v:
                continue
            self.waited[eng][k] = v
            out.append((k, v))
        return out

    def _deps(self, eng, reads, writes):
        ts = []
        for b in reads:
            ts.append(b.w)
            if b.psum:
                for t in b.r:
                    if t[0] != eng:
                        ts.append(t)
        for b in writes:
            ts.append(b.w)
            for t in b.r:
                if t[0] == eng:
                    continue
                ts.append(t)
        return self._need(eng, ts)

    def _register(self, ticket, reads, writes):
        self.seq += 1
        for b in reads:
            b.r.append(ticket)
            b.rel = self.seq
        for b in writes:
            b.w = ticket
            b.r = []
            b.rel = self.seq

    def op(self, eng, fn, reads=(), writes=(), inc=True):
        waits = self._deps(eng, reads, writes)
        if inc:
            self.cnt[eng] += 1
            ticket = (eng, self.cnt[eng])
        else:
            ticket = (eng, self.cnt[eng] + 1)
        self._register(ticket, reads, writes)
        sems = [(self._semof(k), v) for k, v in waits]
        mysem = self.sem[eng]

        def thunk(e):
            for s, v in sems[1:]:
                e.wait_ge(s, v)
            ins = fn(e)
            if sems:
                ins._wait_ge(sems[0][0], sems[0][1])
            if inc:
                ins.then_inc(mysem, 1)
        self.q[eng].append(thunk)
        return ticket

    def dma(self, eng, out, in_, reads=(), writes=(), store=False, **kw):
        kind = "sw" if eng == "pool" else "hw"
        base, n = self.dbase[kind]
        idx = base + self.dnext[kind]
        self.dnext[kind] = (self.dnext[kind] + 1) % n
        prev = self.dval[idx]
        extra = [(("d", idx), prev)] if prev > 0 else []
        waits = self._deps(eng, reads, writes) + self._need(eng, extra)
        self.dval[idx] += 16
        ticket = (("d", idx), self.dval[idx])
        self._register(ticket, reads, writes)
        sems = [(self._semof(k), v) for k, v in waits]
        dsem = self.dsem[idx]

        def thunk(e):
            for s, v in sems:
                e.wait_ge(s, v)
            e.dma_start(out=out, in_=in_, **kw).then_inc(dsem, 16)
        self.q[eng].append(thunk)
        if store:
            self.store_tickets.append(ticket)
        return ticket

    def finish(self):
        nc = self.nc
        final = self._need("sp", self.store_tickets)
        sems = [(self._semof(k), v) for k, v in final]

        def fthunk(e):
            for s, v in sems:
                e.wait_ge(s, v)
        self.q["sp"].append(fthunk)
        block = self.es.enter_context(nc.Block())
        q = self.q

        @block.sync
        def _(e):
            for t in q["sp"]:
                t(e)

        @block.tensor
        def _(e):
            for t in q["pe"]:
                t(e)

        @block.scalar
        def _(e):
            for t in q["act"]:
                t(e)

        @block.vector
        def _(e):
            for t in q["dve"]:
                t(e)

        @block.gpsimd
        def _(e):
            for t in q["pool"]:
                t(e)


class LRU:
    def __init__(self, fw, tiles, life=None, name=""):
        self.fw = fw
        self.tiles = tiles
        self.life = life
        self.name = name
        self.allocs = []
        self.n = 0
        self.busy_until = [0] * len(tiles)

    def get(self):
        self.fw.seq += 1
        i = self.n
        self.n += 1
        if self.life is None:
            t = self.tiles[i % len(self.tiles)]
            b = Buf("rec", psum=t[1].psum)
            self.allocs.append(b)
            return (t[0], b)
        now = self.fw.seq
        cands = [k for k in range(len(self.tiles)) if self.busy_until[k] < now]
        if not cands:
            raise RuntimeError("pool %s exhausted at alloc %d" % (self.name, i))
        k = min(cands, key=lambda k: self.busy_until[k])
        self.busy_until[k] = max(self.life[i], now)
        return self.tiles[k]

    def lifetimes(self):
        return [b.rel for b in self.allocs]


_SUB = int(os.environ.get('KSUB', '9'))

O_BG, O_CG, O_V, O_ZA, O_AB, O_GB, O_ZB, O_Q, O_ZX = 0, 384, 768, 1152, 1536, 1920, 2304, 2688, 2944
D_IN = 3200


def build_nc(stage=99):
    life = _build(stage, None)
    return _build(stage, life)


def _build(stage, life):
    nc = bass.Bass("TRN2", target_bir_lowering=False)

    def din(name, shape):
        return nc.dram_tensor(name, list(shape), F32, kind="ExternalInput").ap()

    def dout(name, shape):
        return nc.dram_tensor(name, list(shape), F32, kind="ExternalOutput").ap()

    xp = din("xp", (2048, 1024))
    xs = din("xs", (128, 1024))
    sca = din("sca", (32, 384))
    scb = din("scb", (16, 30, 384))
    ck = din("ck", (16, 256, 256))
    cv = din("cv", (16, 256, 256))
    mem = din("mem", (256, 1024))
    g_norm = din("g_norm", (1024,))
    w_in = din("w_in", (1024, D_IN))
    w_conv_a = din("w_conv_a", (3, 384))
    b_conv_a = din("b_conv_a", (384,))
    w_conv_b = din("w_conv_b", (31, 384))
    b_conv_b = din("b_conv_b", (384,))
    ln_g = din("ln_g", (384,))
    ln_b = din("ln_b", (384,))
    w_out = din("w_out", (1024, 1024))
    g_mem = din("g_mem", (1024,))
    w_mem_k = din("w_mem_k", (1024, 256))
    w_mem_v = din("w_mem_v", (1024, 256))
    g_final = din("g_final", (1024,))
    ident_d = din("ident", (128, 128))

    yp = dout("yp", (2048, 1024))
    ys = dout("ys", (128, 1024))
    o_pa = dout("o_pa", (2, 384))
    o_pb = dout("o_pb", (30, 384))
    o_pk = dout("o_pk", (256, 256))
    o_pv = dout("o_pv", (256, 256))
    o_sa = dout("o_sa", (32, 384))
    o_sb = dout("o_sb", (16, 30, 384))

    with ExitStack() as es:
        fw = FW(nc, es)

        def T(name, shape, dt):
            return es.enter_context(nc.sbuf_tensor(name, list(shape), dt))

        pools = {}

        def mkpool(prefix, n, shape, dt):
            p = LRU(fw, [(T("%s%d" % (prefix, i), shape, dt), Buf("%s%d" % (prefix, i))) for i in range(n)],
                    None if life is None else life[prefix], prefix)
            pools[prefix] = p
            return p

        win = T("win", (128, 8, D_IN), BF16); b_win = {(g, k): Buf("win%d_%d" % (g, k)) for g in range(6) for k in range(8)}
        wout = T("wout", (128, 8, 1024), BF16); b_wout = [Buf("wout%d" % k) for k in range(8)]
        dgb = T("dgb", (128, 3, 31, 128), BF16); b_dgb = [Buf("dgb%d" % j) for j in range(3)]
        gfin = T("gfin", (128, 1024), F32); b_gfin = Buf("gfin")
        idf = T("idf", (128, 128), F32); b_idf = Buf("idf")
        idb = T("idb", (128, 128), BF16); b_idb = Buf("idb")
        ones = T("ones", (128, 128), BF16); b_ones = Buf("ones")
        gn = T("gn", (128, 8), F32); b_gn = Buf("gn")
        gm = T("gm", (128, 8), F32); b_gm = Buf("gm")
        bcb = T("bcb", (128, 3), F32); lng = T("lng", (128, 3), F32); lnb = T("lnb", (128, 3), F32)
        bca = T("bca", (128, 3), F32); wca = T("wca", (128, 3, 3), F32); wcb = T("wcb", (128, 3, 31), F32)
        nhalf = T("nhalf", (128, 8), F32); b_nhalf = Buf("nhalf")
        expw = T("expw", (128, 2), F32); b_expw = Buf("expw")
        b_gate = Buf("gate")
        hT = [T("hT%d" % i, (128, 8, 512), BF16) for i in range(2)]; b_hT = [Buf("hT0"), Buf("hT1")]
        ubuf = T("ubuf", (128, 3, 544), BF16); b_ub = [Buf("ub%d" % j) for j in range(3)]
        outT = T("outT", (128, 8, 512), BF16); b_outT = [Buf("outT%d" % j) for j in range(8)]
        wkv = hT[0]
        b_wkv = [b_hT[0]]
        cvh = T("cvh", (128, 3, 2), F32); b_cvh = Buf("cvh")
        cvh_s = T("cvh_s", (128, 3, 32), F32); b_cvh_s = Buf("cvh_s")
        ubuf_s = T("ubuf_s", (128, 3, 608), BF16); b_ub_s = [Buf("ubs%d" % j) for j in range(3)]
        u32 = T("u32", (128, 3, 128), F32); b_u32 = Buf("u32")
        KTz = T("KTz", (128, 4, 256), BF16); b_KTp = Buf("KTz")
        Vz = T("Vz", (128, 2, 4, 128), BF16); b_Vp = Buf("Vz")
        onesz = T("onesz", (128, 2, 128), BF16); b_onesz = Buf("onesz")
        qT = T("qT", (128, 2, 512), BF16); b_qT = [Buf("qT0"), Buf("qT1")]
        ssn = T("ssn", (128, 8), F32); b_ssn = [Buf("ssn%d" % i) for i in range(8)]
        rsn = T("rsn", (128, 8), F32); b_rsn = [Buf("rsn%d" % i) for i in range(8)]
        ss2 = T("ss2", (128, 8), F32); rs2 = T("rs2", (128, 8), F32)
        b_ss2 = [Buf("ss2_%d" % i) for i in range(8)]; b_rs2 = [Buf("rs2_%d" % i) for i in range(8)]
        lnt = T("lnt", (128, 4, 4), F32); b_lnt = Buf("lnt")
        szx_s = T("szx_s", (128, 2, 128), F32); b_szx_s = Buf("szx_s")
        qTs = T("qTs", (128, 2, 128), BF16); b_qTs = [Buf("qTs0"), Buf("qTs1")]
        OD_s = T("OD_s", (128, 4, 128), F32); b_OD_s = Buf("OD_s")
        outT_s = T("outT_s", (128, 8, 128), BF16); b_outT_s = [Buf("outTs%d" % j) for j in range(8)]

        xpool = mkpool("xq", 6, (128, 1024), F32)
        s32 = mkpool("sc", 10, (128, 512), F32)
        cvp = mkpool("cvb", 2, (128, 514), F32)
        s16 = mkpool("sb", 10, (128, 1024), BF16)
        banks = LRU(fw, [(es.enter_context(nc.psum_tensor("pb%d" % i, [128, 512], F32)), Buf("pb%d" % i, psum=True)) for i in range(8)],
                    None if life is None else life["banks"], "banks")
        pools["banks"] = banks

        op = fw.op
        dma = fw.dma

        dma("sp", idf[:], ident_d[:, :], writes=[b_idf])
        smalls = [Buf("smalls")]
        dma("act", gfin[:], g_final.partition_broadcast(128), writes=[b_gfin])

        def load_small_consts():
            stg, bstg = xpool.get()
            bs = [bstg] + [Buf("stg%d" % i) for i in range(7)]
            for r, d_ in ((0, b_conv_b), (1, ln_g), (2, ln_b), (3, b_conv_a)):
                dma("sp", stg[r:r + 1, 0:384], d_.rearrange("(o c) -> o c", o=1), writes=[bs[r]])
            dma("sp", stg[4:7, 0:384], w_conv_a[:, :], writes=[bs[4]])
            dma("sp", stg[7:38, 0:384], w_conv_b[:, :], writes=[bs[5]])
            dma("sp", stg[38:46, 0:128], g_norm.rearrange("(k p) -> k p", p=128), writes=[bs[6]])
            dma("sp", stg[46:54, 0:128], g_mem.rearrange("(k p) -> k p", p=128), writes=[bs[7]])
            pb, bpb = banks.get()
            offs = (0, 54, 92)
            for j in range(3):
                R = 54 if j == 0 else 38
                op("pe", lambda e, j=j, R=R: e.transpose(out=pb[:, offs[j]:offs[j] + R], in_=stg[0:R, j * 128:(j + 1) * 128],
                                                         identity=idf[0:R, 0:R]), reads=bs + [b_idf], writes=[bpb], inc=(j == 2))
            cst, bcst = s32.get()
            op("dve", lambda e: e.tensor_copy(out=cst[:, 0:130], in_=pb[:, 0:130]), reads=[bpb], writes=[bcst])

            def col(j, r):
                return offs[j] + r
            for j in range(3):
                for t_, r in ((bcb, 0), (lng, 1), (lnb, 2), (bca, 3)):
                    op("dve", lambda e, t_=t_, j=j, r=r: e.tensor_copy(out=t_[:, j:j + 1], in_=cst[:, col(j, r):col(j, r) + 1]),
                       reads=[bcst], writes=[smalls[0]])
                op("dve", lambda e, j=j: e.tensor_copy(out=wca[:, j, :], in_=cst[:, col(j, 4):col(j, 7)]), reads=[bcst], writes=[smalls[0]])
                op("dve", lambda e, j=j: e.tensor_copy(out=wcb[:, j, :], in_=cst[:, col(j, 7):col(j, 38)]), reads=[bcst], writes=[smalls[0]])
            op("dve", lambda e: e.tensor_copy(out=gn[:], in_=cst[:, 38:46]), reads=[bcst], writes=[b_gn])
            op("dve", lambda e: e.tensor_copy(out=gm[:], in_=cst[:, 46:54]), reads=[bcst], writes=[b_gm])

        op("pool", lambda e: e.memset(nhalf[:], -0.5), writes=[b_nhalf])
        op("pool", lambda e: e.memset(ones[:], 1.0), writes=[b_ones])
        op("dve", lambda e: e.tensor_copy(out=idb[:], in_=idf[:]), reads=[b_idf], writes=[b_idb])

        cast_rr = [0]

        def cast_scaled(dst, src, sc_ap, mul, reads, writes):
            if sc_ap is None:
                eng = "dve" if (cast_rr[0] % 2 == 0) else "act"
                cast_rr[0] += 1
                if eng == "dve":
                    op(eng, lambda e: e.tensor_copy(out=dst, in_=src), reads=reads, writes=writes)
                else:
                    op(eng, lambda e: e.activation(out=dst, in_=src, func=AF.Copy), reads=reads, writes=writes)
            else:
                op("dve", lambda e: e.tensor_scalar(out=dst, in0=src, scalar1=sc_ap, scalar2=float(mul),
                                                    op0=ALU.mult, op1=ALU.mult), reads=reads, writes=writes)

        b_wk, b_wv = Buf("wk"), Buf("wv")

        def issue_wkv(gate=()):
            for wi, (wsrc, bw) in enumerate(((w_mem_k, b_wk), (w_mem_v, b_wv))):
                dma("pool", wkv[:, :, wi * 256:(wi + 1) * 256], wsrc.rearrange("(k p) c -> p k c", p=128), reads=list(gate),
                    writes=[bw], max_dma_last_dim=4096)

        def scale_wkv():
            for kc in range(8):
                op("dve", lambda e, kc=kc: e.tensor_scalar(out=wkv[:, kc, :], in0=wkv[:, kc, :], scalar1=gm[:, kc:kc + 1], scalar2=1.0,
                                                          op0=ALU.mult, op1=ALU.mult), reads=[b_wk, b_wv, b_gm], writes=b_wkv)

        WIN_GROUPS = [
            (O_GB, O_ZB, [(O_GB, O_ZB, 1.0)]),
            (O_AB, O_GB, [(O_AB, O_GB, 0.5)]),
            (O_ZB, O_Q, [(O_ZB, O_Q, 1.0)]),
            (0, O_V, [(0, O_V, 1.0)]),
            (O_V, O_AB, [(O_V, O_AB, 1.0)]),
            (O_Q, D_IN, [(O_Q, O_ZX, 0.125), (O_ZX, D_IN, 1.0)]),
        ]
        w_in_v = w_in.rearrange("(k p) c -> p k c", p=128)

        def issue_win(groups, gate=()):
            for gi in groups:
                c0, c1, subs = WIN_GROUPS[gi]
                dma("pool", win[:, :, c0:c1], w_in_v[:, :, c0:c1], reads=list(gate), writes=[b_win[(gi, kc)] for kc in range(8)],
                    max_dma_last_dim=4096)

        def scale_win(groups):
            for gi in groups:
                c0, c1, subs = WIN_GROUPS[gi]
                for kc in range(8):
                    for (a0, a1, mul) in subs:
                        op("dve", lambda e, kc=kc, a0=a0, a1=a1, mul=mul: e.tensor_scalar(
                            out=win[:, kc, a0:a1], in0=win[:, kc, a0:a1], scalar1=gn[:, kc:kc + 1], scalar2=float(mul),
                            op0=ALU.mult, op1=ALU.mult), reads=[b_win[(gi, kc)], b_gn], writes=[b_win[(gi, kc)]])

        def win_group(col):
            for gi, (c0, c1, subs) in enumerate(WIN_GROUPS):
                if c0 <= col < c1:
                    return gi
            raise ValueError(col)

        def load_wout(gate=()):
            dma("pool", wout[:, :, :], w_out.rearrange("(k p) c -> p k c", p=128), reads=list(gate), writes=list(b_wout),
                max_dma_last_dim=4096)

        def prep_load(src_rows, ntile, gate=()):
            xts = []
            for tt in range(ntile):
                xt, bxt = xpool.get()
                dma("sp", xt[:], src_rows(tt), reads=list(gate), writes=[bxt])
                xts.append((xt, bxt))
            return xts

        def prep_a(src_rows, ntile, xts=None):
            xhs = []
            if xts is None:
                xts = prep_load(src_rows, ntile)
            for tt in range(ntile):
                xt, bxt = xts[tt]
                xh, bxh = s16.get()
                op("act", lambda e, xt=xt, xh=xh, tt=tt: e.activation(out=xh[:], in_=xt[:], func=AF.Square,
                                                                       accum_out=ssn[:, tt:tt + 1]),
                   reads=[bxt], writes=[bxh, b_ssn[tt]])
                op("pool", lambda e, tt=tt: e.tensor_scalar(out=rsn[:, tt:tt + 1], in0=ssn[:, tt:tt + 1], scalar1=1.0 / 1024,
                                                            scalar2=EPS, op0=ALU.mult, op1=ALU.add),
                   reads=[b_ssn[tt]], writes=[b_rsn[tt]])
                op("pool", lambda e, tt=tt: e.tensor_tensor(out=rsn[:, tt:tt + 1], in0=rsn[:, tt:tt + 1], in1=nhalf[:, 0:1], op=ALU.pow),
                   reads=[b_rsn[tt], b_nhalf], writes=[b_rsn[tt]])
                op("dve", lambda e, xt=xt, xh=xh, tt=tt: e.tensor_scalar(out=xh[:], in0=xt[:], scalar1=rsn[:, tt:tt + 1],
                                                                         scalar2=None, op0=ALU.mult),
                   reads=[bxt, b_rsn[tt]], writes=[bxh])
                xhs.append((xh, bxh))
            return xhs

        def prep_b(xhs, dstT, b_dstT, col0):
            for tt, (xh, bxh) in enumerate(xhs):
                pb, bpb = banks.get()
                pbv = pb[:].bitcast(BF16).rearrange("p (k c) -> p k c", c=128)
                for kc in range(8):
                    op("pe", lambda e, pbv=pbv, xh=xh, kc=kc: e.transpose(out=pbv[:, kc, :], in_=xh[:, kc * 128:(kc + 1) * 128],
                                                                         identity=idb[:]),
                       reads=[bxh, b_idb], writes=[bpb], inc=(kc == 7))
                c = col0 + tt * 128
                op("act", lambda e, pbv=pbv, c=c: e.activation(out=dstT[:, :, c:c + 128], in_=pbv, func=AF.Copy),
                   reads=[bpb], writes=[b_dstT])

        def prep_tiles(src_rows, ntile, dstT, b_dstT, col0):
            prep_b(prep_a(src_rows, ntile), dstT, b_dstT, col0)

        def proj(col, hTb, b_hTb, nt, M=128):
            pb, bpb = banks.get()
            gi = win_group(col)
            for kc in range(8):
                op("pe", lambda e, pb=pb, kc=kc: e.matmul(pb[0:M, 0:nt], lhsT=win[:, kc, col:col + M], rhs=hTb[:, kc, 0:nt],
                                                         start=(kc == 0), stop=(kc == 7)),
                   reads=[b_win[(gi, kc)], b_hTb], writes=[bpb], inc=(kc == 7))
            return pb, bpb


        class Ctx:
            pass

        def seqv(ap, width):
            return ap.rearrange("p (s w) -> p s w", w=width)

        def new_ctx(hTb, b_hTb, nt, sample, last, oT, b_oT, qTd, b_qTd, ub=None, b_ub_=None, cvh_=None, b_cvh_=None):
            c = Ctx()
            c.hTb, c.b_hTb, c.nt, c.sample, c.last = hTb, b_hTb, nt, sample, last
            c.oT, c.b_oT, c.qT, c.b_qT = oT, b_oT, qTd, b_qTd
            c.ntile = nt // 128
            c.S = 16 if sample else 1
            c.L = nt // c.S
            c.ub = ubuf if ub is None else ub
            c.b_ub = b_ub if b_ub_ is None else b_ub_
            c.cvh = cvh if cvh_ is None else cvh_
            c.b_cvh = b_cvh if b_cvh_ is None else b_cvh_
            c.szb, c.y32, c.ysq, c.szx = [], [], [], []
            c.hook = lambda: None
            c.gate = None
            c.tdve = T_DVE
            return c

        def phase_B_proj(c):
            nt = c.nt
            ths = []
            for j in range(3):
                pg, bpg = proj(O_GB + 128 * j, c.hTb, c.b_hTb, nt)
                th, bth = s32.get()
                op("act", lambda e, pg=pg, th=th: e.activation(out=th[:, 0:nt], in_=pg[:, 0:nt], func=AF.Tanh, scale=0.5),
                   reads=[bpg], writes=[bth])
                ths.append((th, bth))
            for j in range(3):
                th, bth = ths[j]
                pa, bpa = proj(O_AB + 128 * j, c.hTb, c.b_hTb, nt)
                if c.sample:
                    udst = c.ub[:, j, :].rearrange("p (s w) -> p s w", w=38)[:, :, 30:38]
                    op("dve", lambda e, th=th, pa=pa, udst=udst: e.scalar_tensor_tensor(
                        out=udst, in0=seqv(th[:, 0:nt], 8), scalar=1.0, in1=seqv(pa[:, 0:nt], 8), op0=ALU.add, op1=ALU.mult),
                       reads=[bth, bpa], writes=[c.b_ub[j]])
                    op("dve", lambda e, th=th, pa=pa, j=j: e.scalar_tensor_tensor(
                        out=u32[:, j, 0:128], in0=th[:, 0:128], scalar=1.0, in1=pa[:, 0:128], op0=ALU.add, op1=ALU.mult),
                       reads=[bth, bpa], writes=[b_u32])
                else:
                    op("dve", lambda e, th=th, pa=pa, j=j: e.scalar_tensor_tensor(
                        out=c.ub[:, j, 30:30 + nt], in0=th[:, 0:nt], scalar=1.0, in1=pa[:, 0:nt], op0=ALU.add, op1=ALU.mult),
                       reads=[bth, bpa], writes=[c.b_ub[j]])
                    if c.last:
                        op("dve", lambda e, th=th, pa=pa, j=j: e.scalar_tensor_tensor(
                            out=u32[:, j, 0:32], in0=th[:, nt - 32:nt], scalar=1.0, in1=pa[:, nt - 32:nt],
                            op0=ALU.add, op1=ALU.mult), reads=[bth, bpa], writes=[b_u32])
            for j in range(3):
                pz, bpz = proj(O_ZB + 128 * j, c.hTb, c.b_hTb, nt)
                sz, bsz = s32.get()
                op("act", lambda e, pz=pz, sz=sz: e.activation(out=sz[:, 0:nt], in_=pz[:, 0:nt], func=AF.Silu),
                   reads=[bpz], writes=[bsz])
                c.szb.append((sz, bsz))
                c.hook()

        T_DVE = 6

        def phase_B_conv(c):
            nt = c.nt
            T = 0 if c.sample else c.tdve
            for j in range(3):
                part = None
                if T > 0:
                    part, bpart = s32.get()
                    op("dve", lambda e, part=part, j=j: e.tensor_scalar(out=part[:, 0:nt], in0=c.ub[:, j, 0:nt], scalar1=wcb[:, j, 0:1],
                                                                        scalar2=bcb[:, j:j + 1], op0=ALU.mult, op1=ALU.add),
                       reads=[c.b_ub[j], *smalls], writes=[bpart])
                    for k in range(1, T):
                        op("dve", lambda e, part=part, j=j, k=k: e.scalar_tensor_tensor(
                            out=part[:, 0:nt], in0=c.ub[:, j, k:k + nt], scalar=wcb[:, j, k:k + 1], in1=part[:, 0:nt],
                            op0=ALU.mult, op1=ALU.add), reads=[c.b_ub[j], *smalls, bpart], writes=[bpart])
                pc, bpc = banks.get()
                for k in range(T, 31):
                    if c.sample:
                        rhs = c.ub[:, j, :].rearrange("p (s w) -> p s w", w=38)[:, :, k:k + 8]
                        o = seqv(pc[:, 0:nt], 8)
                    else:
                        rhs = c.ub[:, j, k:k + nt]
                        o = pc[:, 0:nt]
                    op("pe", lambda e, o=o, rhs=rhs, j=j, k=k: e.matmul(o, lhsT=dgb[:, j, k, :], rhs=rhs,
                                                                       start=(k == T), stop=(k == 30)),
                       reads=[b_dgb[j], c.b_ub[j]], writes=[bpc] + ([c.gate] if (c.gate is not None and j == 0 and k == 30) else []),
                       inc=(k == 30))
                yq, byq = s16.get()
                if T > 0:
                    yy, byy = part, bpart
                    op("dve", lambda e, pc=pc, yy=yy: e.tensor_tensor(out=yy[:, 0:nt], in0=yy[:, 0:nt], in1=pc[:, 0:nt], op=ALU.add),
                       reads=[byy, bpc], writes=[byy])
                    op("act", lambda e, yy=yy, yq=yq: e.activation(out=yq[:, 0:nt], in_=yy[:, 0:nt], func=AF.Square),
                       reads=[byy], writes=[byq])
                else:
                    yy, byy = s32.get()
                    op("act", lambda e, pc=pc, yy=yy, j=j: e.activation(out=yy[:, 0:nt], in_=pc[:, 0:nt], func=AF.Identity,
                                                                         bias=bcb[:, j:j + 1]), reads=[bpc, *smalls], writes=[byy])
                    op("act", lambda e, pc=pc, yq=yq, j=j: e.activation(out=yq[:, 0:nt], in_=pc[:, 0:nt], func=AF.Square,
                                                                         bias=bcb[:, j:j + 1]), reads=[bpc, *smalls], writes=[byq])
                op("dve", lambda e, yq=yq, yy=yy: e.tensor_copy(out=yq[:, 512:512 + nt], in_=yy[:, 0:nt]),
                   reads=[byy], writes=[byq])
                c.y32.append((yy, byy))
                c.ysq.append((yq, byq))
                if not c.sample and not c.last:
                    op("pool", lambda e, j=j: e.tensor_copy(out=c.ub[:, j, 0:30], in_=c.ub[:, j, nt:nt + 30]),
                       reads=[], writes=[c.b_ub[j]])
                c.hook()

        def phase_LN(c):
            nt, ntile = c.nt, c.ntile
            pst, bpst = banks.get()
            for tt in range(ntile):
                for which in range(2):
                    for j in range(3):
                        yq, byq = c.ysq[j]
                        src = yq[:, 512 + tt * 128:512 + (tt + 1) * 128] if which == 0 else yq[:, tt * 128:(tt + 1) * 128]
                        op("pe", lambda e, src=src, tt=tt, which=which, j=j: e.matmul(
                            pst[:, 2 * tt + which:2 * tt + which + 1], lhsT=src, rhs=ones[:, 0:1], start=(j == 0), stop=(j == 2)),
                           reads=[byq, b_ones], writes=[bpst], inc=(j == 2 and which == 1 and tt == ntile - 1))
            pstv = pst[:, 0:2 * ntile].rearrange("p (t w) -> p t w", w=2)
            mean = lnt[:, 0:ntile, 0]
            msq = lnt[:, 0:ntile, 1]
            rstd = lnt[:, 0:ntile, 2]
            nb = lnt[:, 0:ntile, 3]
            op("dve", lambda e: e.tensor_scalar(out=mean, in0=pstv[:, :, 0], scalar1=1.0 / 384, scalar2=None, op0=ALU.mult),
               reads=[bpst], writes=[b_lnt])
            op("dve", lambda e: e.tensor_tensor(out=msq, in0=mean, in1=mean, op=ALU.mult), reads=[b_lnt], writes=[b_lnt])
            op("dve", lambda e: e.scalar_tensor_tensor(out=msq, in0=pstv[:, :, 1], scalar=1.0 / 384, in1=msq,
                                                       op0=ALU.mult, op1=ALU.subtract), reads=[bpst, b_lnt], writes=[b_lnt])
            op("pool", lambda e: e.tensor_scalar(out=rstd, in0=msq, scalar1=1.0, scalar2=EPS, op0=ALU.mult, op1=ALU.add),
               reads=[b_lnt], writes=[b_lnt])
            op("pool", lambda e: e.tensor_tensor(out=rstd, in0=rstd, in1=nhalf[:, 0:ntile], op=ALU.pow),
               reads=[b_lnt, b_nhalf], writes=[b_lnt])
            op("dve", lambda e: e.scalar_tensor_tensor(out=nb, in0=mean, scalar=-1.0, in1=rstd, op0=ALU.mult, op1=ALU.mult),
               reads=[b_lnt], writes=[b_lnt])

        def phase_LN_bcast(c):
            nt, ntile = c.nt, c.ntile
            prb, bprb = banks.get()
            pmb, bpmb = banks.get()
            c.prb, c.bprb, c.pmb, c.bpmb = prb, bprb, pmb, bpmb
            for tt in range(ntile):
                op("pe", lambda e, tt=tt: e.matmul(prb[:, tt * 128:(tt + 1) * 128], lhsT=lnt[:, tt, 2:3].to_broadcast([128, 128]),
                                                   rhs=idf[:], start=True, stop=True),
                   reads=[b_lnt, b_idf], writes=[bprb], inc=(tt == ntile - 1))
            for tt in range(ntile):
                op("pe", lambda e, tt=tt: e.matmul(pmb[:, tt * 128:(tt + 1) * 128], lhsT=lnt[:, tt, 3:4].to_broadcast([128, 128]),
                                                   rhs=idf[:], start=True, stop=True),
                   reads=[b_lnt, b_idf], writes=[bpmb], inc=(tt == ntile - 1))

        def phase_LN_elem(c):
            nt = c.nt
            prb, bprb, pmb, bpmb = c.prb, c.bprb, c.pmb, c.bpmb
            for j in range(3):
                yy, byy = c.y32[j]
                sz, bsz = c.szb[j]
                op("dve", lambda e, yy=yy: e.tensor_tensor(out=yy[:, 0:nt], in0=yy[:, 0:nt], in1=prb[:, 0:nt], op=ALU.mult),
                   reads=[byy, bprb], writes=[byy])
                op("dve", lambda e, yy=yy: e.tensor_tensor(out=yy[:, 0:nt], in0=yy[:, 0:nt], in1=pmb[:, 0:nt], op=ALU.add),
                   reads=[byy, bpmb], writes=[byy])
                op("act", lambda e, yy=yy, j=j: e.activation(out=yy[:, 0:nt], in_=yy[:, 0:nt], func=AF.Silu,
                                                              scale=lng[:, j:j + 1], bias=lnb[:, j:j + 1]),
                   reads=[byy, *smalls], writes=[byy])
                op("dve", lambda e, yy=yy, sz=sz, j=j: e.tensor_tensor(out=c.oT[:, 3 + j, 0:nt], in0=yy[:, 0:nt], in1=sz[:, 0:nt],
                                                                       op=ALU.mult), reads=[byy, bsz], writes=[c.b_oT[3 + j]])

        def phase_A(c):
            nt, S, L = c.nt, c.S, c.L
            W = L + 2
            for j in range(3):
                pcg, bpcg = proj(O_CG + 128 * j, c.hTb, c.b_hTb, nt)
                cgs, bcgs = s32.get()
                op("act", lambda e, pcg=pcg, cgs=cgs: e.activation(out=cgs[:, 0:nt], in_=pcg[:, 0:nt], func=AF.Copy),
                   reads=[bpcg], writes=[bcgs])
                pv_, bpv_ = proj(O_V + 128 * j, c.hTb, c.b_hTb, nt)
                cb, bcb_ = cvp.get()
                cbv = cb[:, 0:S * W].rearrange("p (s w) -> p s w", w=W)
                op("dve", lambda e, cbv=cbv, j=j: e.tensor_copy(out=cbv[:, :, 0:2],
                                                                in_=c.cvh[:, j, 0:2 * S].rearrange("p (s w) -> p s w", w=2)),
                   reads=[c.b_cvh], writes=[bcb_])
                op("dve", lambda e, cbv=cbv, cgs=cgs, pv_=pv_: e.tensor_tensor(out=cbv[:, :, 2:W], in0=seqv(cgs[:, 0:nt], L),
                                                                              in1=seqv(pv_[:, 0:nt], L), op=ALU.mult),
                   reads=[bcgs, bpv_], writes=[bcb_])
                op("dve", lambda e, cbv=cbv, j=j: e.tensor_copy(out=c.cvh[:, j, 0:2 * S].rearrange("p (s w) -> p s w", w=2),
                                                                in_=cbv[:, :, L:W]), reads=[bcb_], writes=[c.b_cvh])
                acc, bacc = s32.get()
                accv = seqv(acc[:, 0:nt], L)
                op("dve", lambda e, accv=accv, cbv=cbv, j=j: e.tensor_scalar(out=accv, in0=cbv[:, :, 0:L], scalar1=wca[:, j, 0:1],
                                                                             scalar2=bca[:, j:j + 1], op0=ALU.mult, op1=ALU.add),
                   reads=[bcb_, *smalls], writes=[bacc])
                for k in (1, 2):
                    op("dve", lambda e, accv=accv, cbv=cbv, j=j, k=k: e.scalar_tensor_tensor(
                        out=accv, in0=cbv[:, :, k:k + L], scalar=wca[:, j, k:k + 1], in1=accv, op0=ALU.mult, op1=ALU.add),
                       reads=[bcb_, *smalls, bacc], writes=[bacc])
                pz, bpz = proj(O_ZA + 128 * j, c.hTb, c.b_hTb, nt)
                sz, bsz = s32.get()
                op("act", lambda e, pz=pz, sz=sz: e.activation(out=sz[:, 0:nt], in_=pz[:, 0:nt], func=AF.Silu),
                   reads=[bpz], writes=[bsz])
                pbg, bpbg = proj(O_BG + 128 * j, c.hTb, c.b_hTb, nt)
                op("dve", lambda e, sz=sz, pbg=pbg: e.tensor_tensor(out=sz[:, 0:nt], in0=sz[:, 0:nt], in1=pbg[:, 0:nt], op=ALU.mult),
                   reads=[bsz, bpbg], writes=[bsz])
                op("dve", lambda e, sz=sz, acc=acc, j=j: e.tensor_tensor(out=c.oT[:, j, 0:nt], in0=acc[:, 0:nt], in1=sz[:, 0:nt],
                                                                         op=ALU.mult), reads=[bsz, bacc], writes=[c.b_oT[j]])
                c.hook()

        def phase_X_proj(c):
            nt = c.nt
            for jx in range(2):
                pz, bpz = proj(O_ZX + 128 * jx, c.hTb, c.b_hTb, nt)
                if c.sample:
                    op("act", lambda e, pz=pz, jx=jx: e.activation(out=szx_s[:, jx, 0:nt], in_=pz[:, 0:nt], func=AF.Silu),
                       reads=[bpz], writes=[b_szx_s])
                else:
                    sz, bsz = s32.get()
                    op("act", lambda e, pz=pz, sz=sz: e.activation(out=sz[:, 0:nt], in_=pz[:, 0:nt], func=AF.Silu),
                       reads=[bpz], writes=[bsz])
                    c.szx.append((sz, bsz))
            if not c.sample:
                op("act", lambda e: e.activation(out=expw[:, 0:1], in_=nhalf[:, 0:1], func=AF.Exp), reads=[b_nhalf], writes=[b_expw])
            for jx in range(2):
                pq, bpq = proj(O_Q + 128 * jx, c.hTb, c.b_hTb, nt)
                op("act", lambda e, pq=pq, jx=jx: e.activation(out=c.qT[:, jx, 0:nt], in_=pq[:, 0:nt], func=AF.Copy),
                   reads=[bpq], writes=[c.b_qT[jx]])

        def phase_LN_apply(c):
            phase_LN_bcast(c)
            phase_LN_elem(c)

        def phase_X_attn(c, KV, pairs=(0, 1)):
            nt = c.nt
            KT_, bKT, V_, bV = KV
            pO, pD, pts = {}, {}, {}

            def S(h):
                jx = h // 2
                pt, bpt = s16.get()
                pts[h] = (pt, bpt)
                for mc in range(2):
                    ps_, bps = banks.get()
                    op("pe", lambda e, ps_=ps_, jx=jx, h=h, mc=mc: e.matmul(
                        ps_[:, 0:nt], lhsT=KT_[:, h, mc * 128:(mc + 1) * 128], rhs=c.qT[:, jx, 0:nt],
                        start=True, stop=True), reads=[bKT, c.b_qT[jx]], writes=[bps])
                    op("act", lambda e, ps_=ps_, pt=pt, mc=mc: e.activation(out=pt[:, mc * 512:mc * 512 + nt], in_=ps_[:, 0:nt],
                                                                             func=AF.Exp), reads=[bps], writes=[bpt])

            def PVD(h):
                jx = h // 2
                pt, bpt = pts.pop(h)
                first, last = (h % 2 == 0), (h % 2 == 1)
                if first:
                    pO[jx] = banks.get()
                    pD[jx] = banks.get()
                for mc in range(2):
                    op("pe", lambda e, h=h, jx=jx, mc=mc, pt=pt, first=first, last=last: e.matmul(
                        pO[jx][0][:, 0:nt], lhsT=V_[:, mc, h, :], rhs=pt[:, mc * 512:mc * 512 + nt],
                        start=(first and mc == 0), stop=(last and mc == 1)), reads=[bV, bpt], writes=[pO[jx][1]], inc=(mc == 1))
                for mc in range(2):
                    op("pe", lambda e, h=h, jx=jx, mc=mc, pt=pt, first=first, last=last: e.matmul(
                        pD[jx][0][:, 0:nt], lhsT=onesz[:, h % 2, :], rhs=pt[:, mc * 512:mc * 512 + nt],
                        start=(first and mc == 0), stop=(last and mc == 1)), reads=[b_onesz, bpt], writes=[pD[jx][1]], inc=(mc == 1))
                if not last:
                    return
                rec, brec = s32.get()
                sz, bsz = c.szx[jx]
                op("dve", lambda e, rec=rec, jx=jx: e.reciprocal(out=rec[:, 0:nt], in_=pD[jx][0][:, 0:nt]),
                   reads=[pD[jx][1]], writes=[brec])
                op("dve", lambda e, rec=rec, jx=jx: e.tensor_tensor(out=rec[:, 0:nt], in0=rec[:, 0:nt], in1=pO[jx][0][:, 0:nt],
                                                                    op=ALU.mult), reads=[brec, pO[jx][1]], writes=[brec])
                op("dve", lambda e, rec=rec, sz=sz, jx=jx: e.tensor_tensor(out=c.oT[:, 6 + jx, 0:nt], in0=rec[:, 0:nt], in1=sz[:, 0:nt],
                                                                           op=ALU.mult), reads=[brec, bsz], writes=[c.b_oT[6 + jx]])

            heads = [h for h in range(4) if h // 2 in pairs]
            S(heads[0])
            for i, h in enumerate(heads):
                if i + 1 < len(heads):
                    S(heads[i + 1])
                PVD(h)

        def sample_seq_load(s):
            st, bst = xpool.get()
            stk = st[:, 0:512].rearrange("p (m f) -> p m f", f=256)
            stv = st[:, 512:1024].rearrange("p (m f) -> p m f", f=256)
            dma("sp", stk, ck[s].rearrange("(m p) f -> p m f", p=128), writes=[bst])
            dma("sp", stv, cv[s].rearrange("(m p) f -> p m f", p=128), writes=[bst])
            kvb, bkvb = s16.get()
            op("dve", lambda e: e.tensor_copy(out=kvb[:], in_=st[:]), reads=[bst], writes=[bkvb])
            return kvb, bkvb

        seq_loaded = {}
        seq_started = set()

        def sample_attn_stages(s):
            if s not in seq_loaded:
                seq_loaded[s] = sample_seq_load(s)
            kvb, bkvb = seq_loaded.pop(s)
            for s2 in (s + 1, s + 2):
                if s2 < 16 and s2 not in seq_loaded and s2 not in seq_started:
                    seq_loaded[s2] = sample_seq_load(s2)
            seq_started.add(s)
            ptk, bptk = banks.get()
            ptkv = ptk[:].bitcast(BF16).rearrange("p (j m) -> p j m", m=256)
            for mc in range(2):
                for jx in range(2):
                    op("pe", lambda e, mc=mc, jx=jx: e.transpose(
                        out=ptkv[:, jx, mc * 128:(mc + 1) * 128], in_=kvb[:, mc * 256 + jx * 128:mc * 256 + (jx + 1) * 128],
                        identity=idb[:]), reads=[bkvb, b_idb], writes=[bptk], inc=(mc == 1 and jx == 1))
            kts, bkts = s16.get()
            op("dve", lambda e: e.tensor_copy(out=kts[:, 0:512].rearrange("p (j m) -> p j m", m=256), in_=ptkv[:, 0:2, :]),
               reads=[bptk], writes=[bkts])
            ktsv = kts[:, 0:512].rearrange("p (j m) -> p j m", m=256)
            yield
            pt = kts[:, 512:576]
            bpt_ = Buf("pt_s")
            for par in range(2):
                ps_, bps = banks.get()
                for jx in range(2):
                    po = 64 * par
                    for mc in range(2):
                        cb = (jx * 2 + mc) * 8
                        op("pe", lambda e, ps_=ps_, jx=jx, po=po, mc=mc, cb=cb: e.matmul(
                            ps_[:, cb:cb + 8], lhsT=ktsv[po:po + 64, jx, mc * 128:(mc + 1) * 128],
                            rhs=qTs[po:po + 64, jx, s * 8:(s + 1) * 8], start=True, stop=True),
                           reads=[bkts, b_qTs[jx]], writes=[bps], inc=(jx == 1 and mc == 1))
                op("act", lambda e, ps_=ps_, par=par: e.activation(out=pt[:, par * 32:(par + 1) * 32], in_=ps_[:, 0:32], func=AF.Exp),
                   reads=[bps], writes=[bpt_])
            yield
            pod, bpod = banks.get()
            for h in range(4):
                jx, po = h // 2, 64 * (h % 2)
                for kind in range(2):
                    for mc in range(2):
                        c0 = (h % 2) * 32 + ((h // 2) * 2 + mc) * 8
                        lhsT = kvb[:, 512 + mc * 256 + h * 64:512 + mc * 256 + (h + 1) * 64] if kind == 0 else ones[:, 0:64]
                        oc = kind * 16 + jx * 8
                        op("pe", lambda e, po=po, mc=mc, c0=c0, lhsT=lhsT, oc=oc: e.matmul(
                            pod[po:po + 64, oc:oc + 8], lhsT=lhsT, rhs=pt[:, c0:c0 + 8], start=(mc == 0), stop=(mc == 1)),
                           reads=[bkvb, bkts, bpt_, b_ones], writes=[bpod], inc=(h == 3 and kind == 1 and mc == 1))
            op("dve", lambda e: e.tensor_copy(out=OD_s[:, :, s * 8:(s + 1) * 8], in_=pod[:, 0:32].rearrange("p (a w) -> p a w", w=8)),
               reads=[bpod], writes=[b_OD_s])
            yield

        class Weaver:
            def __init__(self, seqs):
                self.seqs = list(seqs)
                self.act = []
                self.rr = 0

            def _fill(self):
                while len(self.act) < 2 and self.seqs:
                    self.act.append(sample_attn_stages(self.seqs.pop(0)))

            def __call__(self):
                self._fill()
                while self.act:
                    self.rr = (self.rr + 1) % len(self.act)
                    g = self.act[self.rr]
                    try:
                        next(g)
                        return
                    except StopIteration:
                        self.act.remove(g)
                        self._fill()

            def drain(self):
                self._fill()
                while self.act:
                    self()

        def sample_attn_finish():
            for jx in range(2):
                rec, brec = s32.get()
                op("dve", lambda e, rec=rec, jx=jx: e.reciprocal(out=rec[:, 0:128], in_=OD_s[:, 2 + jx, :]),
                   reads=[b_OD_s], writes=[brec])
                op("dve", lambda e, rec=rec, jx=jx: e.tensor_tensor(out=rec[:, 0:128], in0=rec[:, 0:128], in1=OD_s[:, jx, :],
                                                                    op=ALU.mult), reads=[brec, b_OD_s], writes=[brec])
                op("dve", lambda e, rec=rec, jx=jx: e.tensor_tensor(out=outT_s[:, 6 + jx, :], in0=rec[:, 0:128], in1=szx_s[:, jx, :],
                                                                    op=ALU.mult), reads=[brec, b_szx_s], writes=[b_outT_s[6 + jx]])


        slot = [0]

        def wout_block(nt, xrows, yrows, oT, b_oT):
            ntile = nt // 128
            pending = []

            def finish(xr, bxr, sl, tt):
                op("dve", lambda e, xr=xr, sl=sl: e.scalar_tensor_tensor(out=xr[:], in0=xr[:], scalar=rs2[:, sl:sl + 1], in1=gfin[:],
                                                                         op0=ALU.mult, op1=ALU.mult),
                   reads=[bxr, b_rs2[sl], b_gfin], writes=[bxr])
                dma("pool", yrows(tt), xr[:], reads=[bxr], store=True)

            tiles = {}
            for tt0 in range(0, ntile, 2):
                grp = [tt for tt in (tt0, tt0 + 1) if tt < ntile]
                for tt in grp:
                    xr, bxr = xpool.get()
                    dma("sp", xr[:], xrows(tt), writes=[bxr])
                    tiles[tt] = (xr, bxr, [banks.get(), banks.get()])
                for kcs in (range(0, 7), (7,)):
                    for tt in grp:
                        for half in range(2):
                            pb, bpb = tiles[tt][2][half]
                            for kc in kcs:
                                op("pe", lambda e, pb=pb, kc=kc, tt=tt, half=half: e.matmul(
                                    pb[:, 0:512], lhsT=oT[:, kc, tt * 128:(tt + 1) * 128], rhs=wout[:, kc, half * 512:(half + 1) * 512],
                                    start=(kc == 0), stop=(kc == 7)), reads=[b_oT[kc], b_wout[kc]], writes=[bpb], inc=(kc in (6, 7)))
            for tt in range(ntile):
                xr, bxr, bk = tiles[tt]
                sl = slot[0] % 8
                slot[0] += 1
                for half in range(2):
                    pb, bpb = bk[half]
                    op("dve", lambda e, pb=pb, xr=xr, half=half: e.tensor_tensor(
                        out=xr[:, half * 512:(half + 1) * 512], in0=xr[:, half * 512:(half + 1) * 512], in1=pb[:, 0:512], op=ALU.add),
                       reads=[bxr, bpb], writes=[bxr])
                jk, bjk = s16.get()
                op("act", lambda e, xr=xr, sl=sl, jk=jk: e.activation(out=jk[:], in_=xr[:], func=AF.Square, accum_out=ss2[:, sl:sl + 1]),
                   reads=[bxr], writes=[bjk, b_ss2[sl]])
                op("pool", lambda e, sl=sl: e.tensor_scalar(out=rs2[:, sl:sl + 1], in0=ss2[:, sl:sl + 1], scalar1=1.0 / 1024,
                                                            scalar2=EPS, op0=ALU.mult, op1=ALU.add),
                   reads=[b_ss2[sl]], writes=[b_rs2[sl]])
                op("pool", lambda e, sl=sl: e.tensor_tensor(out=rs2[:, sl:sl + 1], in0=rs2[:, sl:sl + 1], in1=nhalf[:, 0:1], op=ALU.pow),
                   reads=[b_rs2[sl], b_nhalf], writes=[b_rs2[sl]])
                if pending:
                    finish(*pending.pop(0))
                pending.append((xr, bxr, sl, tt))
            while pending:
                finish(*pending.pop(0))

        def _program():
            op("dve", lambda e: e.memset(cvh[:], 0.0), writes=[b_cvh])
            for j in range(3):
                op("pool", lambda e, j=j: e.memset(ubuf[:, j, 0:30], 0.0), writes=[b_ub[j]])
            issue_win((0, 1, 2))
            xh0 = prep_a(lambda tt: xp[tt * 128:(tt + 1) * 128, :], 4)
            load_small_consts()
            issue_win((3, 4))
            scale_win((0, 1, 2))
            def build_diag(j):
                op("dve", lambda e, j=j: e.tensor_tensor(out=dgb[:, j, :, :], in0=idf[:].unsqueeze(1).to_broadcast([128, 31, 128]),
                                                         in1=wcb[:, j, :].unsqueeze(2).to_broadcast([128, 31, 128]), op=ALU.mult),
                   reads=[b_idf, *smalls], writes=[b_dgb[j]])
            build_diag(0)
            prep_b(xh0, hT[1], b_hT[1], 0)
            op("pool", lambda e: e.memset(KTz[:], 0.0), writes=[b_KTp])
            op("pool", lambda e: e.memset(Vz[:], 0.0), writes=[b_Vp])
            op("pool", lambda e: e.memset(onesz[:], 0.0), writes=[b_onesz])
            for p_ in range(2):
                op("pool", lambda e, p_=p_: e.memset(onesz[:, p_, 64 * p_:64 * p_ + 64], 1.0), writes=[b_onesz])
            mem_state = {}
            def sample_states():
                sst, bsst = s32.get()
                dma("sp", sst[0:32, 0:384], sca[:, :], writes=[bsst])
                pb, bpb = banks.get()
                for j in range(3):
                    op("pe", lambda e, pb=pb, j=j: e.transpose(out=pb[:, j * 32:(j + 1) * 32], in_=sst[0:32, j * 128:(j + 1) * 128],
                                                               identity=idf[0:32, 0:32]), reads=[bsst, b_idf], writes=[bpb], inc=(j == 2))
                op("dve", lambda e, pb=pb: e.tensor_copy(out=cvh_s[:, :, :], in_=pb[:, 0:96].rearrange("p (j w) -> p j w", w=32)),
                   reads=[bpb], writes=[b_cvh_s])
                for g in range(4):
                    st, bst = s32.get()
                    dma("sp", st[0:120, 0:384], scb[4 * g:4 * g + 4].rearrange("s l c -> (s l) c"), writes=[bst])
                    pb, bpb = banks.get()
                    for j in range(3):
                        op("pe", lambda e, pb=pb, st=st, j=j: e.transpose(out=pb[:, j * 120:(j + 1) * 120], in_=st[0:120, j * 128:(j + 1) * 128],
                                                                          identity=idf[0:120, 0:120]), reads=[bst, b_idf], writes=[bpb],
                           inc=(j == 2))
                    for j in range(3):
                        dst = ubuf_s[:, j, :].rearrange("p (s w) -> p s w", w=38)[:, 4 * g:4 * g + 4, 0:30]
                        op("dve", lambda e, pb=pb, dst=dst, j=j: e.tensor_copy(
                            out=dst, in_=pb[:, j * 120:(j + 1) * 120].rearrange("p (s l) -> p s l", l=30)), reads=[bpb], writes=[b_ub_s[j]])
                dma("pool", o_sb[:, 0:22, :], scb[:, 8:30, :], store=True)


            def prompt_state_a():
                pb, bpb = banks.get()
                for j in range(3):
                    op("pe", lambda e, pb=pb, j=j: e.transpose(out=pb[0:2, j * 128:(j + 1) * 128], in_=cvh[:, j, 0:2], identity=idf[:]),
                       reads=[b_cvh, b_idf], writes=[bpb], inc=(j == 2))
                so, bso = s32.get()
                op("act", lambda e, pb=pb, so=so: e.activation(out=so[0:2, 0:384], in_=pb[0:2, 0:384], func=AF.Copy), reads=[bpb], writes=[bso])
                dma("pool", o_pa[:, :], so[0:2, 0:384], reads=[bso], store=True)

            def prompt_state_b():
                pb, bpb = banks.get()
                for j in range(3):
                    op("pe", lambda e, pb=pb, j=j: e.transpose(out=pb[0:32, j * 128:(j + 1) * 128], in_=u32[:, j, 0:32], identity=idf[:]),
                       reads=[b_u32, b_idf], writes=[bpb], inc=(j == 2))
                so2, bso2 = s32.get()
                op("act", lambda e, pb=pb, so2=so2: e.activation(out=so2[0:32, 0:384], in_=pb[0:32, 0:384], func=AF.Copy), reads=[bpb], writes=[bso2])
                dma("pool", o_pb[:, :], so2[2:32, 0:384], reads=[bso2], store=True)


            def prompt_block(b, wv):
                cur = (b + 1) % 2
                c = new_ctx(hT[cur], b_hT[cur], 512, False, b == 3, outT, b_outT, qT, b_qT)
                st = {"xh": None, "n": 0}
                nxt_rows = lambda tt, b=b: xp[(b + 1) * 512 + tt * 128:(b + 1) * 512 + (tt + 1) * 128, :]

                def hook_b():
                    wv()
                    if st["n"] == 1 and b < 3:
                        st["xh"] = prep_a(nxt_rows, 4)
                    st["n"] += 1
                    wv()
                c.hook = hook_b
                phase_B_proj(c)
                if b == 0:
                    build_diag(1)
                    build_diag(2)
                    scale_win((3, 4))
                if b > 0 and b < 3:
                    prep_b(st["xh"], hT[b % 2], b_hT[b % 2], 0)

                def hook_w():
                    wv()
                    wv()
                c.hook = hook_w
                if b == 0:
                    c.gate = b_gate
                    c.tdve = 0
                phase_B_conv(c)
                if b == 0:
                    issue_win((5,), gate=[b_gate])
                    load_wout(gate=[b_gate])
                    mem_gate = [c.szb[0][1]]
                    mem_xts = prep_load(lambda tt: mem[tt * 128:(tt + 1) * 128, :], 2, gate=mem_gate)
                    issue_wkv(gate=mem_gate)
                if b == 3:
                    prompt_state_b()
                st["a"] = 0

                def hook_a():
                    if st["a"] == 0:
                        phase_LN(c)
                    elif st["a"] == 1:
                        wv()
                        wv()
                    else:
                        wv()
                        wv()
                        phase_LN_bcast(c)
                        phase_LN_elem(c)
                    st["a"] += 1
                c.hook = hook_a
                phase_A(c)
                if b == 3:
                    prompt_state_a()
                if b == 0:
                    mem_state["xh"] = prep_a(lambda tt: mem[tt * 128:(tt + 1) * 128, :], 2, mem_xts)
                    scale_wkv()
                    sample_states()
                    mem_kv()
                    prep_b(st["xh"], hT[b % 2], b_hT[b % 2], 0)
                    scale_win((5,))
                phase_X_proj(c)
                wv()
                if b == 3:
                    wv.drain()
                    sample_attn_finish()
                phase_X_attn(c, (KTz, b_KTp, Vz, b_Vp))
                if b == 0:
                    xs_state["xh"] = prep_a(lambda tt: xs[:, :], 1)
                if b == 3:
                    wout_block(128, lambda tt: xs[:, :], lambda tt: ys[:, :], outT_s, b_outT_s)
                wout_block(512, lambda tt, b=b: xp[b * 512 + tt * 128:b * 512 + (tt + 1) * 128, :],
                           lambda tt, b=b: yp[b * 512 + tt * 128:b * 512 + (tt + 1) * 128, :], outT, b_outT)

            xs_state = {}

            def mem_kv():
                mTt = [s16.get(), s16.get()]
                mTv = [t[0][:, 0:1024].rearrange("p (k c) -> p k c", c=256) for t in mTt]
                for tt, (xh, bxh) in enumerate(mem_state["xh"]):
                    pbm, bpbm = banks.get()
                    pbv = pbm[:].bitcast(BF16).rearrange("p (k c) -> p k c", c=128)
                    for kc in range(8):
                        op("pe", lambda e, pbv=pbv, xh=xh, kc=kc: e.transpose(out=pbv[:, kc, :], in_=xh[:, kc * 128:(kc + 1) * 128],
                                                                             identity=idb[:]),
                           reads=[bxh, b_idb], writes=[bpbm], inc=(kc == 7))
                    for hh in range(2):
                        op("act", lambda e, pbv=pbv, hh=hh, tt=tt: e.activation(out=mTv[hh][:, :, tt * 128:(tt + 1) * 128],
                                                                                in_=pbv[:, 4 * hh:4 * hh + 4, :], func=AF.Copy),
                           reads=[bpbm], writes=[mTt[hh][1]])

                def mT_of(kc):
                    return mTv[kc // 4][:, kc % 4, :], mTt[kc // 4][1]
                for mt in range(2):
                    pb, bpb = banks.get()
                    for kc in range(8):
                        op("pe", lambda e, pb=pb, kc=kc, mt=mt: e.matmul(pb[:, 0:512], lhsT=mT_of(kc)[0][:, mt * 128:(mt + 1) * 128], rhs=wkv[:, kc, :],
                                                                         start=(kc == 0), stop=(kc == 7)), reads=[mT_of(kc)[1]] + b_wkv, writes=[bpb],
                           inc=(kc == 7))
                    kvf, bkvf = s32.get()
                    op("act", lambda e, pb=pb, kvf=kvf: e.activation(out=kvf[:], in_=pb[:, 0:512], func=AF.Copy), reads=[bpb], writes=[bkvf])
                    for h in range(4):
                        po = 64 * (h % 2)
                        op("dve", lambda e, pb=pb, mt=mt, h=h, po=po: e.tensor_copy(out=Vz[:, mt, h, po:po + 64],
                                                                                    in_=pb[:, 256 + h * 64:256 + (h + 1) * 64]),
                           reads=[bpb], writes=[b_Vp])
                    dma("pool", o_pk[mt * 128:(mt + 1) * 128, :], kvf[:, 0:256], reads=[bkvf], store=True)
                    dma("pool", o_pv[mt * 128:(mt + 1) * 128, :], kvf[:, 256:512], reads=[bkvf], store=True)
                for jx in range(2):
                    pb, bpb = banks.get()
                    for kc in range(8):
                        op("pe", lambda e, pb=pb, kc=kc, jx=jx: e.matmul(pb[:, 0:256], lhsT=wkv[:, kc, jx * 128:(jx + 1) * 128], rhs=mT_of(kc)[0][:, 0:256],
                                                                         start=(kc == 0), stop=(kc == 7)), reads=[mT_of(kc)[1]] + b_wkv, writes=[bpb],
                           inc=(kc == 7))
                    for hh in range(2):
                        po = 64 * hh
                        op("act", lambda e, pb=pb, jx=jx, hh=hh, po=po: e.activation(out=KTz[po:po + 64, 2 * jx + hh, :], in_=pb[po:po + 64, 0:256],
                                                                                     func=AF.Copy), reads=[bpb], writes=[b_KTp])


            prompt_block(0, lambda: None)
            for s_ in range(2):
                seq_loaded[s_] = sample_seq_load(s_)
            if stage < 6:
                return
            hs_t, b_hT_s = s16.get()
            hT_s = hs_t[:, 0:1024].rearrange("p (k c) -> p k c", c=128)
            prep_b(xs_state["xh"], hT_s, b_hT_s, 0)
            cs = new_ctx(hT_s, b_hT_s, 128, True, False, outT_s, b_outT_s, qTs, b_qTs, ubuf_s, b_ub_s, cvh_s, b_cvh_s)
            phase_B_proj(cs)
            phase_A(cs)
            phase_B_conv(cs)
            phase_X_proj(cs)
            phase_LN(cs)
            phase_LN_bcast(cs)
            phase_LN_elem(cs)
            pb, bpb = banks.get()
            for j in range(3):
                op("pe", lambda e, pb=pb, j=j: e.transpose(out=pb[0:32, j * 128:(j + 1) * 128], in_=cvh_s[:, j, 0:32], identity=idf[:]),
                   reads=[b_cvh_s, b_idf], writes=[bpb], inc=(j == 2))
            so, bso = s32.get()
            op("act", lambda e, pb=pb, so=so: e.activation(out=so[0:32, 0:384], in_=pb[0:32, 0:384], func=AF.Copy), reads=[bpb], writes=[bso])
            dma("pool", o_sa[:, :], so[0:32, 0:384], reads=[bso], store=True)
            pb, bpb = banks.get()
            for j in range(3):
                op("pe", lambda e, pb=pb, j=j: e.transpose(out=pb[:, j * 128:(j + 1) * 128], in_=u32[:, j, 0:128], identity=idf[:]),
                   reads=[b_u32, b_idf], writes=[bpb], inc=(j == 2))
            so2, bso2 = s32.get()
            op("act", lambda e, pb=pb, so2=so2: e.activation(out=so2[:, 0:384], in_=pb[:, 0:384], func=AF.Copy), reads=[bpb], writes=[bso2])
            for s_ in range(16):
                dma("sp", o_sb[s_, 22:30, :], so2[s_ * 8:(s_ + 1) * 8, 0:384], reads=[bso2], store=True)
            if stage < 8:
                return
            wv = Weaver(range(16))
            for b in range(1, 4):
                prompt_block(b, wv)
        _program()
        if life is None:
            return {k: p.lifetimes() for k, p in pools.items()}
        fw.finish()
    return nc


_NC_CACHE = {}


def kernel(x_prompt, x_sample, state_conv_a, state_conv_b, cache_mem_k, cache_mem_v, mem_prompt,
           g_norm, w_in, w_conv_a, b_conv_a, w_conv_b, b_conv_b, ln_g, ln_b, w_out, g_mem,
           w_mem_k, w_mem_v, g_final):
    f = lambda a: np.ascontiguousarray(np.asarray(a, dtype=np.float32))
    x_prompt, x_sample = f(x_prompt), f(x_sample)
    state_conv_a, state_conv_b = f(state_conv_a), f(state_conv_b)
    cache_mem_k, cache_mem_v, mem_prompt = f(cache_mem_k), f(cache_mem_v), f(mem_prompt)
    if "nc" not in _NC_CACHE:
        _NC_CACHE["nc"] = build_nc()
    nc = _NC_CACHE["nc"]
    shared = dict(
        g_norm=f(g_norm)[0], w_in=f(w_in)[0], w_conv_a=f(w_conv_a)[0], b_conv_a=f(b_conv_a)[0],
        w_conv_b=f(w_conv_b)[0], b_conv_b=f(b_conv_b)[0], ln_g=f(ln_g)[0], ln_b=f(ln_b)[0],
        w_out=f(w_out)[0], g_mem=f(g_mem)[0], w_mem_k=f(w_mem_k)[0], w_mem_v=f(w_mem_v)[0],
        g_final=f(g_final), ident=np.eye(128, dtype=np.float32),
    )
    in_maps = []
    for c in range(NCORES):
        s0, s1 = 16 * c, 16 * (c + 1)
        m = dict(shared)
        m["xp"] = x_prompt[c]
        m["xs"] = np.ascontiguousarray(x_sample[s0:s1].reshape(128, 1024))
        m["sca"] = np.ascontiguousarray(state_conv_a[0, s0:s1].reshape(32, 384))
        m["scb"] = np.ascontiguousarray(state_conv_b[0, s0:s1])
        m["ck"] = np.ascontiguousarray(cache_mem_k[0, s0:s1].reshape(16, 256, 256))
        m["cv"] = np.ascontiguousarray(cache_mem_v[0, s0:s1].reshape(16, 256, 256))
        m["mem"] = mem_prompt[c]
        in_maps.append(m)
    res = run_bass_kernel_spmd(nc, in_maps, core_ids=list(range(NCORES)))
    R = res.results
    y_prompt = np.stack([R[c]["yp"] for c in range(NCORES)], 0)
    y_sample = np.concatenate([R[c]["ys"].reshape(16, 8, 1024) for c in range(NCORES)], 0)
    pa = np.stack([R[c]["o_pa"] for c in range(NCORES)], 0)[None]
    pb = np.stack([R[c]["o_pb"] for c in range(NCORES)], 0)[None]
    pk = np.stack([R[c]["o_pk"].reshape(256, 4, 64) for c in range(NCORES)], 0)[None]
    pv = np.stack([R[c]["o_pv"].reshape(256, 4, 64) for c in range(NCORES)], 0)[None]
    sa = np.concatenate([R[c]["o_sa"].reshape(16, 2, 384) for c in range(NCORES)], 0)[None]
    sb = np.concatenate([R[c]["o_sb"] for c in range(NCORES)], 0)[None]
    return (y_prompt, y_sample, pa, pb, pk, pv, sa, sb)
```
